# Optimizing a Trainium2 kernel written in Bass

```python
import jax, jax.numpy as jnp
from jax import lax
import numpy as np


D_MODEL = 2048
BATCH = 1
SEQ = 8192
DEPTH = 1

CHUNK = 64
N_META = 16
PAD_FRONT = (-N_META) % CHUNK
MIX_WIDTH = D_MODEL
HG_WIDTH = MIX_WIDTH // 2
RET_WIDTH = MIX_WIDTH - HG_WIDTH
HG_HEAD_DIM = 128
HG_HEADS = HG_WIDTH // HG_HEAD_DIM
RET_HEADS = 4
RET_HEAD_DIM = RET_WIDTH // RET_HEADS
D_FF = ((8 * D_MODEL + 3 * 256 - 1) // (3 * 256)) * 256
ROPE_BASE = 10000.0
LN_EPS = 1e-5
HEAD_NORM_EPS = 1e-6
DEEPNORM_ALPHA = (2.0 * DEPTH) ** 0.25
DEEPNORM_BETA = (8.0 * DEPTH) ** -0.25
PROJ_WIDTHS = (HG_WIDTH,) * 4 + (RET_WIDTH,) * 4
PROJ_SPLITS = tuple(int(s) for s in np.cumsum(PROJ_WIDTHS)[:-1])
N_PROJ = sum(PROJ_WIDTHS)

kernel_name = 'hybrid_hgrn2_retention_deepnorm_encoder'


def layer_norm(x, g, b):
    xf = x.astype(jnp.float32)
    mu = jnp.mean(xf, axis=-1, keepdims=True)
    var = jnp.mean(jnp.square(xf - mu), axis=-1, keepdims=True)
    y = (xf - mu) * lax.rsqrt(var + LN_EPS)
    return (y * g.astype(jnp.float32) + b.astype(jnp.float32)).astype(x.dtype)


def rotary(x, pos):
    half = x.shape[-1] // 2
    inv_freq = ROPE_BASE ** (-jnp.arange(half, dtype=jnp.float32) / half)
    ang = pos.astype(jnp.float32)[:, None] * inv_freq[None, :]
    cos = jnp.cos(ang)[None, :, None, :]
    sin = jnp.sin(ang)[None, :, None, :]
    x1, x2 = x[..., :half], x[..., half:]
    return jnp.concatenate([x1 * cos - x2 * sin, x1 * sin + x2 * cos], axis=-1)


def hgrn2_group(q, f_pre, i, g, lb, valid, norm_g):
    b_sz, length, _ = q.shape
    n_chunks = length // CHUNK
    f32 = jnp.float32
    m = valid[None, :, None]
    q = jax.nn.silu(q.astype(f32)) * (HG_HEAD_DIM ** -0.5)
    forget = lb + (1.0 - lb) * jax.nn.sigmoid(f_pre.astype(f32))
    log_f = jnp.where(m, jnp.log(forget), 0.0)
    k = jnp.where(m, 1.0 - forget, 0.0)
    v = i.astype(f32)

    def to_chunks(t):
        return t.reshape(b_sz, n_chunks, CHUNK, HG_HEADS, HG_HEAD_DIM).transpose(1, 0, 3, 2, 4)

    causal = jnp.tril(jnp.ones((CHUNK, CHUNK), dtype=bool))[:, :, None]

    def step(state, inp):
        qc, kc, vc, gc = inp
        cum = jnp.cumsum(gc, axis=2)
        diff = cum[:, :, :, None, :] - cum[:, :, None, :, :]
        decay = jnp.exp(jnp.where(causal, diff, -jnp.inf))
        scores = jnp.einsum('bhtk,bhsk,bhtsk->bhts', qc, kc, decay)
        out = (jnp.einsum('bhts,bhsv->bhtv', scores, vc)
               + jnp.einsum('bhtk,bhkv->bhtv', qc * jnp.exp(cum), state))
        last = cum[:, :, -1:, :]
        state = (jnp.exp(last[:, :, 0, :])[..., None] * state
                 + jnp.einsum('bhsk,bhsv->bhkv', kc * jnp.exp(last - cum), vc))
        return state, out

    s0 = jnp.zeros((b_sz, HG_HEADS, HG_HEAD_DIM, HG_HEAD_DIM), f32)
    _, o = lax.scan(step, s0, (to_chunks(q), to_chunks(k), to_chunks(v), to_chunks(log_f)))
    o = o.transpose(1, 0, 3, 2, 4).reshape(b_sz, length, HG_HEADS, HG_HEAD_DIM)
    o = o * lax.rsqrt(jnp.mean(jnp.square(o), axis=-1, keepdims=True) + HEAD_NORM_EPS)
    o = o.reshape(b_sz, length, HG_WIDTH) * norm_g.astype(f32)
    return o * jax.nn.silu(g.astype(f32))


def retention_group(q, k, v, g, pos, valid, norm_g, norm_b):
    b_sz, length, _ = q.shape
    n_chunks = length // CHUNK
    f32 = jnp.float32
    qh = rotary(q.astype(f32).reshape(b_sz, length, RET_HEADS, RET_HEAD_DIM), pos)
    kh = rotary(k.astype(f32).reshape(b_sz, length, RET_HEADS, RET_HEAD_DIM), pos) * (RET_HEAD_DIM ** -0.5)
    kh = jnp.where(valid[None, :, None, None], kh, 0.0)
    vh = v.astype(f32).reshape(b_sz, length, RET_HEADS, RET_HEAD_DIM)
    log_gamma = jnp.log(1.0 - 2.0 ** (-5.0 - jnp.arange(RET_HEADS, dtype=f32)))

    def to_chunks(t):
        return t.reshape(b_sz, n_chunks, CHUNK, RET_HEADS, RET_HEAD_DIM).transpose(0, 3, 1, 2, 4)

    qc, kc, vc = to_chunks(qh), to_chunks(kh), to_chunks(vh)
    j = jnp.arange(CHUNK, dtype=f32)
    rel = j[:, None] - j[None, :]
    intra_decay = jnp.where(rel[None] >= 0,
                            jnp.exp(jnp.maximum(rel, 0.0)[None] * log_gamma[:, None, None]), 0.0)
    scores = jnp.einsum('bhntd,bhnsd->bhnts', qc, kc) * intra_decay[None, :, None]
    o = jnp.einsum('bhnts,bhnsv->bhntv', scores, vc)
    k_dec = kc * jnp.exp((CHUNK - 1.0 - j)[None, :] * log_gamma[:, None])[None, :, None, :, None]
    kv = jnp.einsum('bhnsd,bhnsv->nbhdv', k_dec, vc)
    chunk_decay = jnp.exp(CHUNK * log_gamma)[None, :, None, None]

    def step(state, kv_n):
        return chunk_decay * state + kv_n, state

    r0 = jnp.zeros((b_sz, RET_HEADS, RET_HEAD_DIM, RET_HEAD_DIM), f32)
    _, r_prev = lax.scan(step, r0, kv)
    q_dec = qc * jnp.exp((j + 1.0)[None, :] * log_gamma[:, None])[None, :, None, :, None]
    o = o + jnp.einsum('bhntd,nbhdv->bhntv', q_dec, r_prev)
    o = o.transpose(0, 2, 3, 1, 4).reshape(b_sz, length, RET_HEADS, RET_HEAD_DIM)
    mu = jnp.mean(o, axis=-1, keepdims=True)
    var = jnp.mean(jnp.square(o - mu), axis=-1, keepdims=True)
    o = ((o - mu) * lax.rsqrt(var + HEAD_NORM_EPS)).reshape(b_sz, length, RET_WIDTH)
    o = o * norm_g.astype(f32) + norm_b.astype(f32)
    return o * jax.nn.silu(g.astype(f32))


def setup_inputs(seed: int = 0) -> dict:
    key = jax.random.key(seed)
    ks = jax.random.split(key, 20)
    f32 = jnp.float32
    nrm = lambda k, shape: jax.random.normal(k, shape, f32)
    beta = DEEPNORM_BETA
    col_scale = np.concatenate([np.full((w,), s, np.float32) for w, s in zip(
        PROJ_WIDTHS, (1.0, 1.0, beta, 1.0, 1.0, 1.0, beta, 1.0))])
    x = nrm(ks[0], (BATCH, SEQ, D_MODEL))
    meta_tokens = nrm(ks[1], (N_META, D_MODEL))
    ln_in_g = 1.0 + 0.02 * nrm(ks[2], (D_MODEL,))
    ln_in_b = 0.02 * nrm(ks[3], (D_MODEL,))
    hg_lower_bounds = 0.1 * nrm(ks[4], (DEPTH + 1, HG_WIDTH))
    w_in = nrm(ks[5], (DEPTH, D_MODEL, N_PROJ)) * (D_MODEL ** -0.5) * jnp.asarray(col_scale)
    hg_norm_g = 1.0 + 0.02 * nrm(ks[6], (DEPTH, HG_WIDTH))
    ret_norm_g = 1.0 + 0.02 * nrm(ks[7], (DEPTH, RET_WIDTH))
    ret_norm_b = 0.02 * nrm(ks[8], (DEPTH, RET_WIDTH))
    w_out = nrm(ks[9], (DEPTH, MIX_WIDTH, D_MODEL)) * (MIX_WIDTH ** -0.5) * beta
    ln1_g = 1.0 + 0.02 * nrm(ks[10], (DEPTH, D_MODEL))
    ln1_b = 0.02 * nrm(ks[11], (DEPTH, D_MODEL))
    w_gate = nrm(ks[12], (DEPTH, D_MODEL, D_FF)) * (D_MODEL ** -0.5) * beta
    w_up = nrm(ks[13], (DEPTH, D_MODEL, D_FF)) * (D_MODEL ** -0.5) * beta
    w_down = nrm(ks[14], (DEPTH, D_FF, D_MODEL)) * (D_FF ** -0.5) * beta
    ln2_g = 1.0 + 0.02 * nrm(ks[15], (DEPTH, D_MODEL))
    ln2_b = 0.02 * nrm(ks[16], (DEPTH, D_MODEL))
    return {'x': x, 'meta_tokens': meta_tokens, 'ln_in_g': ln_in_g, 'ln_in_b': ln_in_b,
            'hg_lower_bounds': hg_lower_bounds, 'w_in': w_in, 'hg_norm_g': hg_norm_g,
            'ret_norm_g': ret_norm_g, 'ret_norm_b': ret_norm_b, 'w_out': w_out,
            'ln1_g': ln1_g, 'ln1_b': ln1_b, 'w_gate': w_gate, 'w_up': w_up, 'w_down': w_down,
            'ln2_g': ln2_g, 'ln2_b': ln2_b}


def reference(x, meta_tokens, ln_in_g, ln_in_b, hg_lower_bounds, w_in, hg_norm_g, ret_norm_g,
              ret_norm_b, w_out, ln1_g, ln1_b, w_gate, w_up, w_down, ln2_g, ln2_b):
    b_sz, seq, _ = x.shape
    length = PAD_FRONT + N_META + seq
    pad = jnp.zeros((b_sz, PAD_FRONT, D_MODEL), x.dtype)
    meta = jnp.broadcast_to(meta_tokens.astype(x.dtype)[None], (b_sz, N_META, D_MODEL))
    h = layer_norm(jnp.concatenate([pad, meta, x], axis=1), ln_in_g, ln_in_b)
    pos = jnp.arange(length, dtype=jnp.int32) - PAD_FRONT
    valid = pos >= 0
    lb_all = jnp.cumsum(jax.nn.softmax(hg_lower_bounds.astype(jnp.float32), axis=0), axis=0)
    for layer in range(DEPTH):
        proj = h @ w_in[layer]
        hg_q, hg_f, hg_i, hg_g, r_q, r_k, r_v, r_g = jnp.split(proj, PROJ_SPLITS, axis=-1)
        o_hg = hgrn2_group(hg_q, hg_f, hg_i, hg_g, lb_all[layer], valid, hg_norm_g[layer])
        o_ret = retention_group(r_q, r_k, r_v, r_g, pos, valid, ret_norm_g[layer], ret_norm_b[layer])
        mix = jnp.concatenate([o_hg, o_ret], axis=-1).astype(h.dtype) @ w_out[layer]
        h = layer_norm(DEEPNORM_ALPHA * h + mix, ln1_g[layer], ln1_b[layer])
        ff = (jax.nn.silu(h @ w_gate[layer]) * (h @ w_up[layer])) @ w_down[layer]
        h = layer_norm(DEEPNORM_ALPHA * h + ff, ln2_g[layer], ln2_b[layer])
    return h[:, PAD_FRONT + N_META:, :]
```

```python
import contextlib
import numpy as np
import concourse.bass as bass
import concourse.mybir as mybir
from concourse.bass_utils import run_bass_kernel_spmd

F32 = mybir.dt.float32
BF16 = mybir.dt.bfloat16
AF = mybir.ActivationFunctionType
ALU = mybir.AluOpType
AX = mybir.AxisListType

D = 2048
KC = D // 128
NM = 16
HGH = 8
RH = 4
NPROJ = 8192
LN_EPS = 1e-5
HN_EPS = 1e-6
ALPHA = 2.0 ** 0.25
GAMMAS = [1.0 - 2.0 ** (-5.0 - h) for h in range(RH)]


class Res:
    __slots__ = ("name", "w", "r")

    def __init__(self, name=""):
        self.name = name
        self.w = None
        self.r = []


class Sched:
    def __init__(self, nc, stack, milestones=None):
        self.nc = nc
        self.ms_rec = {}
        self.ms_idx = None
        if milestones is not None:
            self.ms_idx = {k: {c: i + 1 for i, c in enumerate(sorted(v))} for k, v in milestones.items()}
        self.eng = {"pe": nc.tensor, "act": nc.scalar, "dve": nc.vector,
                    "pool": nc.gpsimd, "sp": nc.sync}
        self.sems = {}
        self.cnt = {}
        self.seen = {k: {} for k in self.eng}
        self.stack = stack
        for k in ("pe", "act", "dve", "pool"):
            self.new_sem(k)

    def new_sem(self, key):
        s = self.stack.enter_context(self.nc.semaphore("s_" + key))
        self.sems[key] = s
        self.cnt[key] = 0
        return key

    def _waits(self, e, deps):
        need = {}
        for d in deps:
            if d is None:
                continue
            k, v = d
            if e == "pe" and k == "pe":
                continue
            if k not in self.eng:
                v = self.cnt[k]
            if v > need.get(k, 0):
                need[k] = v
        seen = self.seen[e]
        for k, v in need.items():
            if seen.get(k, 0) >= v:
                continue
            if k in self.eng:
                self.ms_rec.setdefault(k, set()).add(v)
                vv = self.ms_idx[k][v] if self.ms_idx is not None else v
            else:
                vv = v
            self.eng[e].wait_ge(self.sems[k], vv)
            seen[k] = v

    @staticmethod
    def _collect(r, w):
        deps = []
        for x in r:
            deps.append(x.w)
        for x in w:
            deps.append(x.w)
            deps.extend(x.r)
        return deps

    @staticmethod
    def _commit(dep, r, w):
        for x in r:
            x.r.append(dep)
        for x in w:
            x.w = dep
            x.r = []

    def op(self, e, fn, r=(), w=()):
        self._waits(e, self._collect(r, w))
        ins = fn(self.eng[e])
        self.cnt[e] += 1
        if self.ms_idx is None or self.cnt[e] in self.ms_idx.get(e, ()):
            ins.then_inc(self.sems[e], 1)
        self._commit((e, self.cnt[e]), r, w)

    def dma(self, q, semkey, out, in_, r=(), w=(), **kw):
        self._waits(q, self._collect(r, w))
        ins = self.eng[q].dma_start(out=out, in_=in_, **kw)
        self.cnt[semkey] += 16
        ins.then_inc(self.sems[semkey], 16)
        self._commit((semkey, self.cnt[semkey]), r, w)

    def barrier(self, engines=("pe", "act", "dve", "pool", "sp")):
        for e in engines:
            self._waits(e, [(k, v) for k, v in self.cnt.items() if v > 0 and k != e])


class Ring:
    def __init__(self, S, st, nc, name, n, shape, dt, dma=False):
        self.S = S
        self.tiles = [st.enter_context(nc.sbuf_tensor(f"sb_{name}{i}", shape, dt)) for i in range(n)]
        self.res = [Res(f"{name}{i}") for i in range(n)]
        self.sem = [S.new_sem(f"{name}{i}") for i in range(n)] if dma else None
        self.i = 0

    def next(self):
        i = self.i % len(self.tiles)
        self.i += 1
        return self.tiles[i], self.res[i], (self.sem[i] if self.sem else None)


class WLoader:
    def __init__(self, S, st, nc, name, kc, ncol, n_stage, n_bf, kc_stage=None):
        self.S = S
        self.kc = kc
        self.ncol = ncol
        self.kcs = kc_stage or kc
        self.stage = Ring(S, st, nc, name + "_st", n_stage, [128, self.kcs, ncol], F32, dma=True)
        self.bf = Ring(S, st, nc, name + "_bf", n_bf, [128, kc, ncol], BF16)
        self.k = 0

    def load(self, src):
        S = self.S
        bt, br, _ = self.bf.next()
        for k0 in range(0, self.kc, self.kcs):
            stt, sr, ssem = self.stage.next()
            S.dma("sp", ssem, stt[:], src[:, k0:k0 + self.kcs, :], w=[sr])
            e = ("pool", "act", "dve")[self.k % 3]
            self.k += 1
            if e == "act":
                S.op(e, lambda g, a=bt[:, k0:k0 + self.kcs, :], b=stt[:]: g.copy(out=a, in_=b), r=[sr], w=[br])
            else:
                S.op(e, lambda g, a=bt[:, k0:k0 + self.kcs, :], b=stt[:]: g.tensor_copy(out=a, in_=b), r=[sr], w=[br])
        return bt, br


class Prefetch:
    def __init__(self, loader, srcs, depth):
        self.L = loader
        self.srcs = srcs
        self.depth = depth
        self.loaded = []
        self.i = 0
        for _ in range(min(depth, len(srcs))):
            self.loaded.append(self.L.load(self.srcs[len(self.loaded)]))

    def get(self):
        if len(self.loaded) < len(self.srcs):
            self.loaded.append(self.L.load(self.srcs[len(self.loaded)]))
        t = self.loaded[self.i]
        self.i += 1
        return t


def build(T, DFF, NCORES, DBG=False, milestones=None, ms_out=None):
    NT = T // 128
    NCH = T // 64
    TT = NM + T
    FC = DFF // 128
    FG = 4
    NG = FC // FG
    assert FC % FG == 0
    HALF = min(512, T)
    NHALF = T // HALF
    W_AG = 1024 + 8 + 2048
    SOFF, DOFF, ROFF = 0, 1024, 1032

    nc = bass.Bass("TRN2", target_bir_lowering=False)
    di = lambda n, sh: nc.dram_tensor(n, sh, F32, kind="ExternalInput").ap()
    x_d = di("x", [T, D])
    meta_d = di("meta", [NM, D])
    lng_d = di("ln_in_g", [D]); lnb_d = di("ln_in_b", [D])
    lb_d = di("lb2", [2, 1024])
    win_d = di("w_in", [D, NPROJ])
    hgn_d = di("hg_norm_g", [1024])
    rng_d = di("ret_norm_g", [1024]); rnb_d = di("ret_norm_b", [1024])
    wout_d = di("w_out", [D, D])
    l1g_d = di("ln1_g", [D]); l1b_d = di("ln1_b", [D])
    wg_d = di("w_gate", [D, DFF]); wu_d = di("w_up", [D, DFF]); wd_d = di("w_down", [DFF, D])
    l2g_d = di("ln2_g", [D]); l2b_d = di("ln2_b", [D])
    cs_d = di("cs", [128, 2, TT])
    rmask_d = di("rmask", [128, RH, 128])
    kdec_d = di("kdec", [128, RH])
    kdecf_d = di("kdecf", [128, NT, RH])
    kdecm_d = di("kdecm", [NM, RH])
    epsr_d = di("epsr", [128, RH])
    hmask_d = di("hmask", [64, 64])
    rst_d = di("rst", [128, TT])
    sel_d = di("sel", [128, NCORES])
    ident_d = di("ident", [128, 128])
    out_d = nc.dram_tensor("out", [T, D], F32, kind="ExternalOutput").ap()
    if DBG:
        dbg_mix = nc.dram_tensor("dbg_mix", [128, KC, T], BF16, kind="ExternalOutput").ap()
        dbg_start = nc.dram_tensor("dbg_start", [128, 3072], F32, kind="ExternalOutput").ap()
        dbg_h0T = nc.dram_tensor("dbg_h0T", [128, KC, TT], BF16, kind="ExternalOutput").ap()
    ag_in = nc.dram_tensor("ag_in", [128, W_AG], F32)
    ag_out = nc.dram_tensor("ag_out", [NCORES * 128, W_AG], F32)

    win_v = win_d.rearrange("(kc p) n -> p kc n", p=128)

    with contextlib.ExitStack() as st0:
        S = Sched(nc, st0, milestones)
        op, dma = S.op, S.dma

        def sb(stk, name, shape, dt=F32):
            return stk.enter_context(nc.sbuf_tensor("sb_" + name, shape, dt))

        S.new_sem("ld")
        S.new_sem("st")
        S.new_sem("cc")
        ident_f = sb(st0, "ident_f", [128, 128]); ident = sb(st0, "ident", [128, 128], BF16)
        mixT = sb(st0, "mixT", [128, KC, T], BF16)
        stm = contextlib.ExitStack()
        st0_real = st0
        st0 = stm
        h0T = sb(st0, "h0T", [128, KC, TT], BF16)
        gin = sb(st0, "gin", [128, KC]); bin_ = sb(st0, "bin", [128, KC])
        lb = sb(st0, "lb", [128, HGH]); oml = sb(st0, "oml", [128, HGH]); lbt = sb(st0, "lbt", [128, 2, HGH])
        hgn = sb(st0, "hgn", [128, HGH]); rng_ = sb(st0, "rng", [128, 2 * RH]); rnb = sb(st0, "rnb", [128, 2 * RH])
        cs = sb(st0, "cs", [128, 2, TT])
        rmask = sb(st0, "rmask", [128, RH, 128]); kdec = sb(st0, "kdec", [128, RH])
        kdecf = sb(st0, "kdecf", [128, NT, RH]); kdecm = sb(st0, "kdecm", [NM, RH])
        epsr = sb(st0, "epsr", [128, RH]); hmask = sb(st0, "hmask", [64, 64])
        rst = sb(st0, "rst", [128, TT]); sel = sb(st0, "sel", [128, NCORES])
        sstart = sb(st0, "sstart", [128, 1024]); rstart = sb(st0, "rstart", [128, 2048])
        st0 = st0_real
        banks = [st0.enter_context(nc.psum_tensor(f"bank{i}", [128, 512], F32)) for i in range(8)]
        bres = [Res(f"bank{i}") for i in range(8)]
        R_const = Res("const")
        R_h0T = [Res(f"h0T{i}") for i in range(NT + 1)]
        R_mix = [Res(f"mix{i}") for i in range(KC)]
        R_ag = Res("agbuf"); R_smeta = Res("smeta"); R_rmeta = Res("rmeta")
        R_start = Res("start")
        R_agin = Res("agin"); R_agout = Res("agout"); R_out = Res("out")

        def ld(dst, src, **kw):
            dma("sp", "ld", dst, src, w=[R_const], **kw)

        ld(ident_f[:], ident_d)
        ld(gin[:], lng_d.rearrange("(kc p) -> p kc", p=128), allow_slow_non_contiguous=True)
        ld(bin_[:], lnb_d.rearrange("(kc p) -> p kc", p=128), allow_slow_non_contiguous=True)
        ld(lbt[:], lb_d.rearrange("a (h p) -> p a h", p=128), allow_slow_non_contiguous=True)
        ld(hgn[:], hgn_d.rearrange("(h p) -> p h", p=128), allow_slow_non_contiguous=True)
        ld(rng_[:], rng_d.rearrange("(h p) -> p h", p=128), allow_slow_non_contiguous=True)
        ld(rnb[:], rnb_d.rearrange("(h p) -> p h", p=128), allow_slow_non_contiguous=True)
        ld(cs[:], cs_d); ld(rmask[:], rmask_d); ld(kdec[:], kdec_d); ld(kdecf[:], kdecf_d)
        ld(kdecm[:], kdecm_d); ld(epsr[:], epsr_d); ld(hmask[:], hmask_d); ld(rst[:], rst_d); ld(sel[:], sel_d)
        op("dve", lambda e: e.tensor_copy(out=ident[:], in_=ident_f[:]), r=[R_const], w=[R_const])
        op("dve", lambda e: e.tensor_tensor(out=lb[:], in0=lbt[:, 0, :], in1=lbt[:, 1, :], op=ALU.subtract), r=[R_const], w=[R_const])
        op("act", lambda e: e.activation(out=lb[:], in_=lb[:], func=AF.Sigmoid), r=[R_const], w=[R_const])
        op("dve", lambda e: e.tensor_scalar(out=oml[:], in0=lb[:], scalar1=-1.0, scalar2=1.0, op0=ALU.mult, op1=ALU.add), r=[R_const], w=[R_const])
        S.barrier()

        def transpose_to(ps_ap, src_ap, k):
            return lambda e: e.matmul(ps_ap, lhsT=src_ap, rhs=ident[0:k, 0:k], start=True, stop=True)

        def ln_stats(stk_tiles, src, rows, rsrc, eps_bias=LN_EPS):
            stt, mv, rstd, rr = stk_tiles
            for c in range(4):
                op("dve", lambda e, c=c: e.bn_stats(out=stt[:rows, c, :], in_=src[:rows, c * 512:(c + 1) * 512]), r=[rsrc], w=[rr])
            op("dve", lambda e: e.bn_aggr(out=mv[:rows, :], in_=stt[:rows].rearrange("p c s -> p (c s)")), r=[rr], w=[rr])
            op("act", lambda e: e.activation(out=rstd[:rows, :], in_=mv[:rows, 1:2], func=AF.Ln, bias=eps_bias), r=[rr], w=[rr])
            op("act", lambda e: e.activation(out=rstd[:rows, :], in_=rstd[:rows, :], func=AF.Exp, scale=-0.5), r=[rr], w=[rr])
            return mv, rstd, rr

        with contextlib.ExitStack() as st:
            xr = Ring(S, st, nc, "xt", 2, [128, D], F32, dma=True)
            xhr = Ring(S, st, nc, "xh", 2, [128, D], BF16)
            lnt = [(sb(st, f"lnst{i}", [128, 4, 6]), sb(st, f"lnmv{i}", [128, 2]), sb(st, f"lnrs{i}", [128, 1]), Res()) for i in range(2)]
            for ti in range(NT + 1):
                rows = NM if ti == 0 else 128
                col0 = 0 if ti == 0 else NM + (ti - 1) * 128
                src = meta_d if ti == 0 else x_d[(ti - 1) * 128: ti * 128, :]
                xt, xres, xsem = xr.next()
                dma("sp", xsem, xt[:rows, :], src, w=[xres])
                mv, rstd, rr = ln_stats(lnt[ti % 2], xt, rows, xres)
                xh, xhres, _ = xhr.next()
                op("dve", lambda e: e.tensor_scalar(out=xh[:rows, :], in0=xt[:rows, :], scalar1=mv[:rows, 0:1], scalar2=rstd[:rows, 0:1], op0=ALU.subtract, op1=ALU.mult), r=[xres, rr], w=[xhres])
                for g4 in range(4):
                    b = g4 % 2
                    for j in range(4):
                        dc = g4 * 4 + j
                        op("pe", transpose_to(banks[b][:, j * 128: j * 128 + rows], xh[:rows, dc * 128:(dc + 1) * 128], rows), r=[xhres, R_const], w=[bres[b]])
                    for j in range(4):
                        dc = g4 * 4 + j
                        if b == 0:
                            op("act", lambda e, j=j, dc=dc: e.activation(out=h0T[:, dc, col0:col0 + rows], in_=banks[b][:, j * 128: j * 128 + rows], func=AF.Identity, scale=gin[:, dc:dc + 1], bias=bin_[:, dc:dc + 1]), r=[bres[b], R_const], w=[R_h0T[ti]])
                        else:
                            op("dve", lambda e, j=j, dc=dc: e.tensor_scalar(out=h0T[:, dc, col0:col0 + rows], in0=banks[b][:, j * 128: j * 128 + rows], scalar1=gin[:, dc:dc + 1], scalar2=bin_[:, dc:dc + 1], op0=ALU.mult, op1=ALU.add), r=[bres[b], R_const], w=[R_h0T[ti]])
            S.barrier()

        def h0res(tok0, n):
            out = []
            for ti in range(NT + 1):
                a = 0 if ti == 0 else NM + (ti - 1) * 128
                b_ = NM if ti == 0 else a + 128
                if a < tok0 + n and b_ > tok0:
                    out.append(R_h0T[ti])
            return out

        def proj(wt, wr, blocks):
            for (tok0, n, b) in blocks:
                for kc in range(KC):
                    op("pe", lambda e, kc=kc: e.matmul(banks[b][:, 0:n], lhsT=wt[:, kc, :], rhs=h0T[:, kc, tok0:tok0 + n], start=(kc == 0), stop=(kc == KC - 1)),
                       r=[wr] + h0res(tok0, n), w=[bres[b]])

        main_blocks = lambda b0: [(NM + i * HALF, HALF, b0 + i) for i in range(NHALF)]

        with contextlib.ExitStack() as st:
            WL = WLoader(S, st, nc, "win", KC, 128, 2, 5)
            wcol = lambda c0: win_v[:, :, c0:c0 + 128]
            specs = []
            for h in range(HGH):
                specs += [wcol(1024 + h * 128), wcol(2048 + h * 128)]
            for h in range(RH):
                specs += [wcol(5 * 1024 + h * 256), wcol(5 * 1024 + h * 256 + 128), wcol(6 * 1024 + h * 256), wcol(6 * 1024 + h * 256 + 128)]
            for h in range(HGH):
                specs += [wcol(1 * 1024 + h * 128), wcol(0 * 1024 + h * 128), wcol(2 * 1024 + h * 128), wcol(3 * 1024 + h * 128)]
            for h in range(RH):
                specs += [wcol(g * 1024 + h * 256 + a_ * 128) for g in (4, 5, 6, 7) for a_ in range(2)]
            PFW = None
            fm32 = [sb(st, f"fm32_{i}", [128, TT]) for i in range(5)]
            fmr = [Res() for _ in range(5)]
            fmb = [sb(st, f"fmb_{i}", [128, TT], BF16) for i in range(8)]
            fbr = [Res() for _ in range(8)]
            TMW = max((NCH + 1) * 128, (NT + 1) * 256)
            tmk_f = sb(st, "tmk", [128, TMW], BF16); r_tmk = Res()
            tmv_f = sb(st, "tmv", [128, TMW], BF16); r_tmv = Res()

            class _TM:
                def __init__(self, t):
                    self.t = t
                    self.w = 128

                def __getitem__(self, key):
                    p, ci, c = key
                    return self.t[p, ci * self.w + c.start: ci * self.w + c.stop]
            tmk = _TM(tmk_f); tmv = _TM(tmv_f)
            small = sb(st, "small", [128, 64]); r_small = Res()
            S32 = sb(st, "S32", [128, 512]); Sbf = sb(st, "Sbf", [128, 512], BF16); r_S = Res(); r_Sbf = Res()
            scm = sb(st, "scm", [128, 128], BF16); r_scm = Res()
            xhc = sb(st, "xhc", [128, 256], BF16); r_xhc = Res()
            junk = sb(st, "junk", [128, 256]); r_junk = Res()
            cst = sb(st, "cst", [128, 6]); cmv = sb(st, "cmv", [128, 2]); crs = sb(st, "crs", [128, 2]); r_c = Res()
            tmp32 = sb(st, "tmp32", [128, 128]); r_tmp = Res()
            S.new_sem("gb0")

            stp1 = contextlib.ExitStack()
            agbuf = sb(stp1, "agbuf", [128, W_AG])
            smeta = sb(stp1, "smeta", [128, 1024]); rmeta = sb(stp1, "rmeta", [128, 2048])
            if DBG:
                print("SBUF remaining in mixer stage:", nc.sbuf_bytes_remaining)

            def evac_copy(i, dst, src, r, w):
                if i % 2 == 0:
                    op("act", lambda e: e.copy(out=dst, in_=src), r=r, w=w)
                else:
                    op("dve", lambda e: e.tensor_copy(out=dst, in_=src), r=r, w=w)

            def hg_f_chain(h, blocks, ncols, c0, want_kt):
                sig, lgf, kk, cum = fm32[0], fm32[1], fm32[2], fm32[4]
                for (tok0, n, b) in blocks:
                    a = tok0 - c0
                    op("act", lambda e, a=a, n=n, b=b: e.activation(out=sig[:, a:a + n], in_=banks[b][:, 0:n], func=AF.Sigmoid), r=[bres[b]], w=[fmr[0]])
                    op("act", lambda e, a=a, n=n, b=b: e.activation(out=kk[:, a:a + n], in_=banks[b][:, 0:n], func=AF.Sigmoid, scale=-1.0), r=[bres[b]], w=[fmr[2]])
                op("act", lambda e: e.activation(out=lgf[:, 0:ncols], in_=sig[:, 0:ncols], func=AF.Ln, scale=oml[:, h:h + 1], bias=lb[:, h:h + 1]), r=[fmr[0], R_const], w=[fmr[1]])
                op("dve", lambda e: e.tensor_tensor_scan(out=cum[:, 0:ncols], data0=rst[:, c0:c0 + ncols], data1=lgf[:, 0:ncols], initial=0.0, op0=ALU.mult, op1=ALU.add), r=[fmr[1], R_const], w=[fmr[4]])
                chunks = []
                a = 0
                if c0 == 0:
                    chunks.append((0, NM)); a = NM
                while a < ncols:
                    chunks.append((a, 64)); a += 64
                nch = len(chunks)
                for ci, (a, n) in enumerate(chunks):
                    pass
                if c0 == 0:
                    op("dve", lambda e: e.tensor_copy(out=small[:, 0:1], in_=cum[:, NM - 1:NM]), r=[fmr[4]], w=[r_small])
                    mc0, k0 = NM, 1
                else:
                    mc0, k0 = 0, 0
                nmain = (ncols - mc0) // 64
                cm = cum[:, mc0:ncols].rearrange("p (c j) -> p c j", j=64)
                op("dve", lambda e: e.tensor_copy(out=small[:, k0:k0 + nmain], in_=cm[:, :, 63]), r=[fmr[4]], w=[r_small])
                op("act", lambda e: e.activation(out=small[:, 32:32 + nch], in_=small[:, 0:nch], func=AF.Exp), r=[r_small], w=[r_small])
                if c0 == 0:
                    op("dve", lambda e: e.tensor_scalar(out=sig[:, 0:NM], in0=cum[:, 0:NM], scalar1=small[:, 0:1], scalar2=None, op0=ALU.subtract), r=[fmr[4], r_small], w=[fmr[0]])
                sm = sig[:, mc0:ncols].rearrange("p (c j) -> p c j", j=64)
                op("dve", lambda e: e.tensor_tensor(out=sm, in0=cm, in1=small[:, k0:k0 + nmain].unsqueeze(2).to_broadcast([128, nmain, 64]), op=ALU.subtract), r=[fmr[4], r_small], w=[fmr[0]])
                op("act", lambda e: e.activation(out=sig[:, 0:ncols], in_=sig[:, 0:ncols], func=AF.Exp, scale=-1.0), r=[fmr[0]], w=[fmr[0]])
                op("dve", lambda e: e.scalar_tensor_tensor(out=fmb[0][:, 0:ncols], in0=kk[:, 0:ncols], scalar=oml[:, h:h + 1], in1=sig[:, 0:ncols], op0=ALU.mult, op1=ALU.mult), r=[fmr[2], fmr[0], R_const], w=[fbr[0]])
                if want_kt:
                    op("act", lambda e: e.activation(out=fm32[3][:, 0:ncols], in_=cum[:, 0:ncols], func=AF.Exp), r=[fmr[4]], w=[fmr[3]])
                    op("act", lambda e: e.activation(out=lgf[:, 0:ncols], in_=cum[:, 0:ncols], func=AF.Exp, scale=-1.0), r=[fmr[4]], w=[fmr[1]])
                    op("dve", lambda e: e.scalar_tensor_tensor(out=fmb[1][:, 0:ncols], in0=kk[:, 0:ncols], scalar=oml[:, h:h + 1], in1=lgf[:, 0:ncols], op0=ALU.mult, op1=ALU.mult), r=[fmr[2], fmr[1], R_const], w=[fbr[1]])
                return chunks

            def to_tm(srcs, chunks, dst, rdst, width):
                i = 0
                for ci, (a, n) in enumerate(chunks):
                    b = 4 + (ci % 2)
                    for j, (sap, sres) in enumerate(srcs):
                        op("pe", transpose_to(banks[b][0:n, j * 128:(j + 1) * 128], sap[:, a:a + n], 128), r=[sres, R_const], w=[bres[b]])
                    evac_copy(ci, dst[0:n, ci, 0:width], banks[b][0:n, 0:width], [bres[b]], [rdst])

            all_blocks = [(0, NM, 2)] + main_blocks(0)
            for h in range(HGH):
                if PFW is None:
                    PFW = Prefetch(WL, specs, 2)
                wf, wfr = PFW.get()
                proj(wf, wfr, all_blocks)
                chunks = hg_f_chain(h, all_blocks, TT, 0, False)
                wi, wir = PFW.get()
                proj(wi, wir, [(0, NM, 3)] + main_blocks(6))
                for (tok0, n, b) in [(0, NM, 3)] + main_blocks(6):
                    evac_copy(b, fmb[2][:, tok0:tok0 + n], banks[b][:, 0:n], [bres[b]], [fbr[2]])
                to_tm([(fmb[0], fbr[0])], chunks, tmk, r_tmk, 128)
                to_tm([(fmb[2], fbr[2])], chunks, tmv, r_tmv, 128)
                for ci, (a, n) in enumerate(chunks):
                    b = 4 + (ci % 2)
                    op("pe", lambda e, ci=ci, n=n, b=b: e.matmul(banks[b][:, 0:128], lhsT=tmk[0:n, ci, 0:128], rhs=tmv[0:n, ci, 0:128], start=True, stop=True), r=[r_tmk, r_tmv], w=[bres[b]])
                    if ci == 0:
                        op("dve", lambda e, b=b: e.tensor_copy(out=smeta[:, h * 128:(h + 1) * 128], in_=banks[b][:, 0:128]), r=[bres[b]], w=[R_smeta])
                    elif ci == 1:
                        op("dve", lambda e, b=b: e.tensor_copy(out=S32[:, 0:128], in_=banks[b][:, 0:128]), r=[bres[b]], w=[r_S])
                    else:
                        op("dve", lambda e, b=b, ci=ci: e.scalar_tensor_tensor(out=S32[:, 0:128], in0=S32[:, 0:128], scalar=small[:, 32 + ci:33 + ci], in1=banks[b][:, 0:128], op0=ALU.mult, op1=ALU.add), r=[bres[b], r_S, r_small], w=[r_S])
                op("act", lambda e: e.copy(out=agbuf[:, SOFF + h * 128: SOFF + (h + 1) * 128], in_=S32[:, 0:128]), r=[r_S], w=[R_ag])
                op("dve", lambda e: e.tensor_reduce(out=small[:, 30:31], in_=small[:, 1:1 + NCH], axis=AX.X, op=ALU.add), r=[r_small], w=[r_small])
                op("act", lambda e: e.activation(out=agbuf[:, DOFF + h:DOFF + h + 1], in_=small[:, 30:31], func=AF.Exp), r=[r_small], w=[R_ag])

            def rotary(bk, c0, ncols, d1, d2, blocks):
                for (tok0, n, b1), (_, _, b2) in zip(blocks[0], blocks[1]):
                    a = tok0 - c0
                    cosv = cs[:, 0, tok0:tok0 + n]; sinv = cs[:, 1, tok0:tok0 + n]
                    op("dve", lambda e: e.tensor_tensor(out=fm32[0][:, a:a + n], in0=banks[b1][:, 0:n], in1=cosv, op=ALU.mult), r=[bres[b1], R_const], w=[fmr[0]])
                    op("dve", lambda e: e.tensor_tensor(out=fm32[1][:, a:a + n], in0=banks[b2][:, 0:n], in1=sinv, op=ALU.mult), r=[bres[b2], R_const], w=[fmr[1]])
                    op("dve", lambda e: e.tensor_tensor(out=fm32[2][:, a:a + n], in0=banks[b1][:, 0:n], in1=sinv, op=ALU.mult), r=[bres[b1], R_const], w=[fmr[2]])
                    op("dve", lambda e: e.tensor_tensor(out=fm32[3][:, a:a + n], in0=banks[b2][:, 0:n], in1=cosv, op=ALU.mult), r=[bres[b2], R_const], w=[fmr[3]])
                op("pool", lambda e: e.tensor_tensor(out=fmb[d1][:, 0:ncols], in0=fm32[0][:, 0:ncols], in1=fm32[1][:, 0:ncols], op=ALU.subtract), r=[fmr[0], fmr[1]], w=[fbr[d1]])
                op("pool", lambda e: e.tensor_tensor(out=fmb[d2][:, 0:ncols], in0=fm32[2][:, 0:ncols], in1=fm32[3][:, 0:ncols], op=ALU.add), r=[fmr[2], fmr[3]], w=[fbr[d2]])

            rchunks = [(0, NM)] + [(NM + i * 128, 128) for i in range(NT)]
            tmk.w = tmv.w = 256
            for h in range(RH):
                blk1 = [(0, NM, 2)] + main_blocks(0)
                blk2 = [(0, NM, 3)] + main_blocks(6)
                wk1, wk1r = PFW.get()
                proj(wk1, wk1r, blk1)
                wk2, wk2r = PFW.get()
                proj(wk2, wk2r, blk2)
                rotary(None, 0, TT, 0, 1, (blk1, blk2))
                wv1, wv1r = PFW.get()
                proj(wv1, wv1r, blk1)
                for (tok0, n, b) in blk1:
                    evac_copy(b, fmb[2][:, tok0:tok0 + n], banks[b][:, 0:n], [bres[b]], [fbr[2]])
                wv2, wv2r = PFW.get()
                proj(wv2, wv2r, blk2)
                for (tok0, n, b) in blk2:
                    evac_copy(b + 1, fmb[3][:, tok0:tok0 + n], banks[b][:, 0:n], [bres[b]], [fbr[3]])
                for ci, (a, n) in enumerate(rchunks):
                    b = 4 + (ci % 2)
                    for j in range(2):
                        op("pe", transpose_to(banks[b][0:n, j * 128:(j + 1) * 128], fmb[j][:, a:a + n], 128), r=[fbr[j], R_const], w=[bres[b]])
                    sc_ap = kdecm[0:n, h:h + 1] if ci == 0 else kdecf[:, ci - 1, h:h + 1]
                    op("dve", lambda e, b=b, n=n, ci=ci, sc_ap=sc_ap: e.tensor_scalar(out=tmk[0:n, ci, 0:256], in0=banks[b][0:n, 0:256], scalar1=sc_ap, scalar2=None, op0=ALU.mult), r=[bres[b], R_const], w=[r_tmk])
                to_tm([(fmb[2], fbr[2]), (fmb[3], fbr[3])], rchunks, tmv, r_tmv, 256)
                for a_ in range(2):
                    op("pe", lambda e, a_=a_: e.matmul(banks[4][:, a_ * 256:(a_ + 1) * 256], lhsT=tmk[0:NM, 0, a_ * 128:(a_ + 1) * 128], rhs=tmv[0:NM, 0, 0:256], start=True, stop=True), r=[r_tmk, r_tmv], w=[bres[4]])
                op("act", lambda e: e.copy(out=rmeta[:, h * 512:(h + 1) * 512], in_=banks[4][:, 0:512]), r=[bres[4]], w=[R_rmeta])
                for a_ in range(2):
                    b = 5 + a_
                    for ci in range(1, NT + 1):
                        op("pe", lambda e, a_=a_, ci=ci, b=b: e.matmul(banks[b][:, 0:256], lhsT=tmk[:, ci, a_ * 128:(a_ + 1) * 128], rhs=tmv[:, ci, 0:256], start=(ci == 1), stop=(ci == NT)), r=[r_tmk, r_tmv], w=[bres[b]])
                    evac_copy(a_, agbuf[:, ROFF + h * 512 + a_ * 256: ROFF + h * 512 + (a_ + 1) * 256], banks[b][:, 0:256], [bres[b]], [R_ag])

            dma("sp", "st", ag_in.ap(), agbuf[:], r=[R_ag], w=[R_agin])
            S._waits("pool", [R_agin.w])
            ins = nc.gpsimd.collective_compute("AllGather", ALU.bypass, replica_groups=[list(range(NCORES))], ins=[ag_in.ap()], outs=[ag_out.ap()])
            ins.then_inc(S.sems["cc"], 1)
            S.cnt["cc"] += 1
            R_agout.w = ("cc", S.cnt["cc"])
            op("dve", lambda e: e.tensor_scalar(out=sstart[:], in0=smeta[:], scalar1=sel[:, 0:1], scalar2=None, op0=ALU.mult), r=[R_smeta, R_const], w=[R_start])
            op("pool", lambda e: e.tensor_scalar(out=rstart[:], in0=rmeta[:], scalar1=sel[:, 0:1], scalar2=None, op0=ALU.mult), r=[R_rmeta, R_const], w=[R_start])
            for j in range(NCORES - 1):
                gb, gr, gs = agbuf, R_ag, "gb0"
                dma("sp", gs, gb[:], ag_out.ap()[j * 128:(j + 1) * 128, :], r=[R_agout], w=[gr])
                sm3 = smeta[:].rearrange("p (h v) -> p h v", v=128)
                op("dve", lambda e, gb=gb: e.tensor_tensor(out=sm3, in0=sm3, in1=gb[:, DOFF:DOFF + 8].unsqueeze(2).to_broadcast([128, 8, 128]), op=ALU.mult), r=[gr, R_smeta], w=[R_smeta])
                op("dve", lambda e, gb=gb: e.tensor_tensor(out=smeta[:], in0=smeta[:], in1=gb[:, SOFF:SOFF + 1024], op=ALU.add), r=[gr, R_smeta], w=[R_smeta])
                op("dve", lambda e, j=j: e.scalar_tensor_tensor(out=sstart[:], in0=smeta[:], scalar=sel[:, j + 1:j + 2], in1=sstart[:], op0=ALU.mult, op1=ALU.add), r=[R_smeta, R_const, R_start], w=[R_start])
                for h in range(RH):
                    op("dve", lambda e, gb=gb, h=h: e.scalar_tensor_tensor(out=rmeta[:, h * 512:(h + 1) * 512], in0=rmeta[:, h * 512:(h + 1) * 512], scalar=float(GAMMAS[h] ** T), in1=gb[:, ROFF + h * 512: ROFF + (h + 1) * 512], op0=ALU.mult, op1=ALU.add), r=[gr, R_rmeta], w=[R_rmeta])
                op("dve", lambda e, j=j: e.scalar_tensor_tensor(out=rstart[:], in0=rmeta[:], scalar=sel[:, j + 1:j + 2], in1=rstart[:], op0=ALU.mult, op1=ALU.add), r=[R_rmeta, R_const, R_start], w=[R_start])

            S.barrier()
            stp1.close()
            stp2 = contextlib.ExitStack()
            Sall = sb(stp2, "Sall", [128, 4096], BF16); r_Sall = Res()
            S32b = sb(stp2, "S32b", [128, 512]); r_S2 = [r_S, Res()]
            S32p = [S32, S32b]
            junkA = sb(stp2, "junkA", [128, 1024]); xhcA = sb(stp2, "xhcA", [128, 1024], BF16)
            scmA = sb(stp2, "scmA", [128, 512], BF16)
            tmpA = [sb(stp2, f"tmpA{i}", [128, 512]) for i in range(2)]; r_tmpA = [Res(), Res()]
            cstA = sb(stp2, "cstA", [128, 4, 6]); cmvA = sb(stp2, "cmvA", [128, 4, 2]); crsA = sb(stp2, "crsA", [128, 16])
            if DBG:
                print("SBUF remaining in pass-2 stage:", nc.sbuf_bytes_remaining)
            mb = main_blocks(0)
            hchunks = [(i * 64, 64) for i in range(NCH)]
            tmk.w = tmv.w = 128
            for h in range(HGH):
                wf, wfr = PFW.get()
                proj(wf, wfr, mb)
                hg_f_chain(h, mb, T, NM, True)
                mb2 = main_blocks(6)
                wq, wqr = PFW.get()
                proj(wq, wqr, mb2)
                for (tok0, n, b) in mb2:
                    a = tok0 - NM
                    op("act", lambda e, a=a, n=n, b=b: e.activation(out=fm32[0][:, a:a + n], in_=banks[b][:, 0:n], func=AF.Silu), r=[bres[b]], w=[fmr[0]])
                op("dve", lambda e: e.scalar_tensor_tensor(out=fmb[3][:, 0:T], in0=fm32[0][:, 0:T], scalar=float(128.0 ** -0.5), in1=fm32[3][:, 0:T], op0=ALU.mult, op1=ALU.mult), r=[fmr[0], fmr[3]], w=[fbr[3]])
                wi, wir = PFW.get()
                proj(wi, wir, mb)
                for (tok0, n, b) in mb:
                    evac_copy(b, fmb[2][:, tok0 - NM:tok0 - NM + n], banks[b][:, 0:n], [bres[b]], [fbr[2]])
                wgt, wgr = PFW.get()
                proj(wgt, wgr, mb2)
                for (tok0, n, b) in mb2:
                    a = tok0 - NM
                    op("act", lambda e, a=a, n=n, b=b: e.activation(out=fmb[4][:, a:a + n], in_=banks[b][:, 0:n], func=AF.Silu), r=[bres[b]], w=[fbr[4]])
                to_tm([(fmb[0], fbr[0])], hchunks, tmk, r_tmk, 128)
                to_tm([(fmb[2], fbr[2])], hchunks, tmv, r_tmv, 128)
                kt, qt = fmb[1], fmb[3]
                GC = min(8, NCH)
                op("dve", lambda e: e.tensor_copy(out=S32p[0][:, 0:128], in_=sstart[:, h * 128:(h + 1) * 128]), r=[R_start], w=[r_S2[0]])
                op("act", lambda e: e.copy(out=Sall[:, 0:128], in_=sstart[:, h * 128:(h + 1) * 128]), r=[R_start], w=[r_Sall])
                for g0 in range(0, NCH - 1, 4):
                    nb_ = min(4, NCH - 1 - g0)
                    b = 4 if (g0 // 4) % 2 == 0 else 7
                    for i in range(nb_):
                        ci = g0 + i
                        op("pe", lambda e, ci=ci, i=i: e.matmul(banks[b][:, i * 128:(i + 1) * 128], lhsT=tmk[0:64, ci, 0:128], rhs=tmv[0:64, ci, 0:128], start=True, stop=True), r=[r_tmk, r_tmv], w=[bres[b]])
                    for i in range(nb_):
                        ci = g0 + i
                        src, dst = S32p[ci % 2], S32p[(ci + 1) % 2]
                        op("dve", lambda e, ci=ci, i=i: e.scalar_tensor_tensor(out=dst[:, 0:128], in0=src[:, 0:128], scalar=small[:, 32 + ci:33 + ci], in1=banks[b][:, i * 128:(i + 1) * 128], op0=ALU.mult, op1=ALU.add), r=[bres[b], r_S2[ci % 2], r_small], w=[r_S2[(ci + 1) % 2]])
                        op("pool", lambda e, ci=ci: e.tensor_copy(out=Sall[:, (ci + 1) * 128:(ci + 2) * 128], in_=dst[:, 0:128]), r=[r_S2[(ci + 1) % 2]], w=[r_Sall])
                for g0 in range(0, NCH, GC):
                    for i in range(GC):
                        a = (g0 + i) * 64
                        op("pe", lambda e, a=a, i=i: e.matmul(banks[4][0:64, i * 64:(i + 1) * 64], lhsT=kt[:, a:a + 64], rhs=qt[:, a:a + 64], start=True, stop=True), r=[fbr[1], fbr[3]], w=[bres[4]])
                    op("dve", lambda e: e.tensor_tensor(out=scmA[0:64, 0:GC * 64].rearrange("p (c j) -> p c j", j=64), in0=banks[4][0:64, 0:GC * 64].rearrange("p (c j) -> p c j", j=64), in1=hmask[:].unsqueeze(1).to_broadcast([64, GC, 64]), op=ALU.mult), r=[bres[4], R_const], w=[r_scm])
                    for i in range(GC):
                        ci = g0 + i
                        a = ci * 64
                        b = 5 + i // 4
                        c0 = (i % 4) * 128
                        op("pe", lambda e, ci=ci, i=i, b=b, c0=c0: e.matmul(banks[b][0:64, c0:c0 + 128], lhsT=scmA[0:64, i * 64:(i + 1) * 64], rhs=tmv[0:64, ci, 0:128], start=True, stop=False), r=[r_scm, r_tmv], w=[bres[b]])
                        op("pe", lambda e, ci=ci, a=a, b=b, c0=c0: e.matmul(banks[b][0:64, c0:c0 + 128], lhsT=qt[:, a:a + 64], rhs=Sall[:, ci * 128:(ci + 1) * 128], start=False, stop=True), r=[fbr[3], r_Sall], w=[bres[b]])
                    nbk = (GC + 3) // 4
                    for k in range(nbk):
                        w_ = min(4, GC - 4 * k) * 128
                        op("act", lambda e, k=k, w_=w_: e.activation(out=junkA[0:64, k * 512:k * 512 + w_], in_=banks[5 + k][0:64, 0:w_], func=AF.Square), r=[bres[5 + k]], w=[r_junk])
                    op("dve", lambda e: e.tensor_reduce(out=crsA[0:64, 0:GC], in_=junkA[0:64, 0:GC * 128].rearrange("p (c v) -> p c v", v=128), axis=AX.X, op=ALU.add), r=[r_junk], w=[r_c])
                    op("act", lambda e: e.activation(out=crsA[0:64, 8:8 + GC], in_=crsA[0:64, 0:GC], func=AF.Ln, scale=1.0 / 128.0, bias=HN_EPS), r=[r_c], w=[r_c])
                    op("act", lambda e: e.activation(out=crsA[0:64, 8:8 + GC], in_=crsA[0:64, 8:8 + GC], func=AF.Exp, scale=-0.5), r=[r_c], w=[r_c])
                    for k in range(nbk):
                        nck = min(4, GC - 4 * k)
                        w_ = nck * 128
                        op("dve", lambda e, k=k, w_=w_, nck=nck: e.tensor_tensor(out=xhcA[0:64, k * 512:k * 512 + w_].rearrange("p (c v) -> p c v", v=128), in0=banks[5 + k][0:64, 0:w_].rearrange("p (c v) -> p c v", v=128), in1=crsA[0:64, 8 + 4 * k:8 + 4 * k + nck].unsqueeze(2).to_broadcast([64, nck, 128]), op=ALU.mult), r=[bres[5 + k], r_c], w=[r_xhc])
                    for i in range(GC):
                        op("pe", transpose_to(banks[7][:, i * 64:(i + 1) * 64], xhcA[0:64, i * 128:(i + 1) * 128], 64), r=[r_xhc, R_const], w=[bres[7]])
                    t0_, nt_ = g0 * 64, GC * 64
                    op("dve", lambda e: e.scalar_tensor_tensor(out=mixT[:, h, t0_:t0_ + nt_], in0=banks[7][:, 0:nt_], scalar=hgn[:, h:h + 1], in1=fmb[4][:, t0_:t0_ + nt_], op0=ALU.mult, op1=ALU.mult), r=[bres[7], fbr[4], R_const], w=[R_mix[h]])

            tchunks = [(i * 128, 128) for i in range(NT)]
            tmk.w = tmv.w = 256
            for h in range(RH):
                mb2 = main_blocks(6)
                blk = ([(t0, n, b) for (t0, n, b) in mb], [(t0, n, b) for (t0, n, b) in mb2])
                wq1 = PFW.get(); proj(wq1[0], wq1[1], mb)
                wq2 = PFW.get(); proj(wq2[0], wq2[1], mb2)
                rotary(None, NM, T, 0, 1, blk)
                wk1 = PFW.get(); proj(wk1[0], wk1[1], mb)
                wk2 = PFW.get(); proj(wk2[0], wk2[1], mb2)
                rotary(None, NM, T, 2, 3, blk)
                wv1 = PFW.get(); proj(wv1[0], wv1[1], mb)
                for (tok0, n, b) in mb:
                    evac_copy(b, fmb[4][:, tok0 - NM:tok0 - NM + n], banks[b][:, 0:n], [bres[b]], [fbr[4]])
                wv2 = PFW.get(); proj(wv2[0], wv2[1], mb2)
                for (tok0, n, b) in mb2:
                    evac_copy(b + 1, fmb[5][:, tok0 - NM:tok0 - NM + n], banks[b][:, 0:n], [bres[b]], [fbr[5]])
                wg1 = PFW.get(); proj(wg1[0], wg1[1], mb)
                for (tok0, n, b) in mb:
                    a = tok0 - NM
                    op("act", lambda e, a=a, n=n, b=b: e.activation(out=fmb[6][:, a:a + n], in_=banks[b][:, 0:n], func=AF.Silu), r=[bres[b]], w=[fbr[6]])
                wg2 = PFW.get(); proj(wg2[0], wg2[1], mb2)
                for (tok0, n, b) in mb2:
                    a = tok0 - NM
                    op("act", lambda e, a=a, n=n, b=b: e.activation(out=fmb[7][:, a:a + n], in_=banks[b][:, 0:n], func=AF.Silu), r=[bres[b]], w=[fbr[7]])
                to_tm([(fmb[4], fbr[4]), (fmb[5], fbr[5])], tchunks, tmv, r_tmv, 256)
                op("dve", lambda e: e.tensor_copy(out=S32[:, 0:512], in_=rstart[:, h * 512:(h + 1) * 512]), r=[R_start], w=[r_S])
                op("act", lambda e: e.copy(out=Sbf[:, 0:512], in_=rstart[:, h * 512:(h + 1) * 512]), r=[R_start], w=[r_Sbf])
                g128 = float(GAMMAS[h] ** 128)
                for ci, (a, n) in enumerate(tchunks):
                    q1, q2, k1, k2 = fmb[0], fmb[1], fmb[2], fmb[3]
                    op("pe", lambda e, a=a: e.matmul(banks[2][:, 0:128], lhsT=k1[:, a:a + 128], rhs=q1[:, a:a + 128], start=True, stop=False), r=[fbr[2], fbr[0]], w=[bres[2]])
                    op("pe", lambda e, a=a: e.matmul(banks[2][:, 0:128], lhsT=k2[:, a:a + 128], rhs=q2[:, a:a + 128], start=False, stop=True), r=[fbr[3], fbr[1]], w=[bres[2]])
                    op("dve", lambda e: e.tensor_tensor(out=scm[:, 0:128], in0=banks[2][:, 0:128], in1=rmask[:, h, :], op=ALU.mult), r=[bres[2], R_const], w=[r_scm])
                    op("pe", lambda e, ci=ci: e.matmul(banks[3][:, 0:256], lhsT=scm[:, 0:128], rhs=tmv[:, ci, 0:256], start=True, stop=False), r=[r_scm, r_tmv], w=[bres[3]])
                    op("pe", lambda e, a=a: e.matmul(banks[3][:, 0:256], lhsT=q1[:, a:a + 128], rhs=Sbf[:, 0:256], start=False, stop=False), r=[fbr[0], r_Sbf], w=[bres[3]])
                    op("pe", lambda e, a=a: e.matmul(banks[3][:, 0:256], lhsT=q2[:, a:a + 128], rhs=Sbf[:, 256:512], start=False, stop=True), r=[fbr[1], r_Sbf], w=[bres[3]])
                    if ci < NT - 1:
                        for j, kk_ in enumerate((k1, k2)):
                            op("pe", transpose_to(banks[7][:, j * 128:(j + 1) * 128], kk_[:, a:a + 128], 128), r=[fbr[2 + j], R_const], w=[bres[7]])
                        op("act", lambda e: e.activation(out=tmk[:, 0, 0:256], in_=banks[7][:, 0:256], func=AF.Copy, scale=kdec[:, h:h + 1]), r=[bres[7], R_const], w=[r_tmk])
                        for a_ in range(2):
                            op("pe", lambda e, a_=a_, ci=ci: e.matmul(banks[5][:, a_ * 256:(a_ + 1) * 256], lhsT=tmk[:, 0, a_ * 128:(a_ + 1) * 128], rhs=tmv[:, ci, 0:256], start=True, stop=True), r=[r_tmk, r_tmv], w=[bres[5]])
                        op("dve", lambda e: e.scalar_tensor_tensor(out=S32[:, 0:512], in0=S32[:, 0:512], scalar=g128, in1=banks[5][:, 0:512], op0=ALU.mult, op1=ALU.add), r=[bres[5], r_S], w=[r_S])
                        op("pool", lambda e: e.tensor_copy(out=Sbf[:, 0:512], in_=S32[:, 0:512]), r=[r_S], w=[r_Sbf])
                    op("dve", lambda e: e.bn_stats(out=cst[:, 0:6], in_=banks[3][:, 0:256]), r=[bres[3]], w=[r_c])
                    op("dve", lambda e: e.bn_aggr(out=cmv[:, 0:2], in_=cst[:, 0:6]), r=[r_c], w=[r_c])
                    op("act", lambda e: e.activation(out=crs[:, 1:2], in_=cmv[:, 1:2], func=AF.Ln, bias=epsr[:, h:h + 1]), r=[r_c, R_const], w=[r_c])
                    op("act", lambda e: e.activation(out=crs[:, 1:2], in_=crs[:, 1:2], func=AF.Exp, scale=-0.5), r=[r_c], w=[r_c])
                    op("dve", lambda e: e.tensor_scalar(out=xhc[:, 0:256], in0=banks[3][:, 0:256], scalar1=cmv[:, 0:1], scalar2=crs[:, 1:2], op0=ALU.subtract, op1=ALU.mult), r=[bres[3], r_c], w=[r_xhc])
                    for a_ in range(2):
                        op("pe", transpose_to(banks[4][:, a_ * 128:(a_ + 1) * 128], xhc[:, a_ * 128:(a_ + 1) * 128], 128), r=[r_xhc, R_const], w=[bres[4]])
                    for a_ in range(2):
                        idx = h * 2 + a_
                        op("dve", lambda e, a_=a_, idx=idx: e.tensor_scalar(out=tmp32[:, 0:128], in0=banks[4][:, a_ * 128:(a_ + 1) * 128], scalar1=rng_[:, idx:idx + 1], scalar2=rnb[:, idx:idx + 1], op0=ALU.mult, op1=ALU.add), r=[bres[4], R_const], w=[r_tmp])
                        op("pool", lambda e, a_=a_, idx=idx, a=a: e.tensor_tensor(out=mixT[:, 8 + idx, a:a + 128], in0=tmp32[:, 0:128], in1=fmb[6 + a_][:, a:a + 128], op=ALU.mult), r=[r_tmp, fbr[6 + a_]], w=[R_mix[8 + idx]])
            S.barrier()
            if DBG:
                dma("sp", "st", dbg_mix, mixT[:], w=[R_out])
                dma("sp", "st", dbg_start[:, 0:1024], sstart[:], w=[R_out])
                dma("sp", "st", dbg_start[:, 1024:3072], rstart[:], w=[R_out])
                dma("sp", "st", dbg_h0T, h0T[:], w=[R_out])
                S.barrier()
            stp2.close()
        stm.close()

        with contextlib.ExitStack() as st:
            y = sb(st, "y", [128, NT, D]); r_y = [Res() for _ in range(NT)]
            gB = sb(st, "gB", [128, D]); bB = sb(st, "bB", [128, D]); r_gb = Res()
            lnt = [(sb(st, f"l2st{i}", [128, 4, 6]), sb(st, f"l2mv{i}", [128, 2]), sb(st, f"l2rs{i}", [128, 1]), Res()) for i in range(2)]
            S.new_sem("gbld")
            with contextlib.ExitStack() as st4:
                xr = Ring(S, st4, nc, "x4t", 2, [128, D], F32, dma=True)
                WO = WLoader(S, st4, nc, "wo", KC, 512, 2, 2, kc_stage=4)
                dma("sp", "gbld", gB[:], lng_d.partition_broadcast(128), w=[r_gb])
                dma("sp", "gbld", bB[:], lnb_d.partition_broadcast(128), w=[r_gb])
                op("pool", lambda e: e.tensor_scalar(out=bB[:], in0=bB[:], scalar1=float(ALPHA), scalar2=None, op0=ALU.mult), r=[r_gb], w=[r_gb])
                for ti in range(NT):
                    xt, xres, xsem = xr.next()
                    dma("sp", xsem, xt[:], x_d[ti * 128:(ti + 1) * 128, :], w=[xres])
                    mv, rstd, rr = ln_stats(lnt[ti % 2], xt, 128, xres)
                    op("dve", lambda e: e.tensor_scalar(out=rstd[:], in0=rstd[:], scalar1=float(ALPHA), scalar2=None, op0=ALU.mult), r=[rr], w=[rr])
                    op("dve", lambda e: e.tensor_scalar(out=y[:, ti, :], in0=xt[:], scalar1=mv[:, 0:1], scalar2=rstd[:, 0:1], op0=ALU.subtract, op1=ALU.mult), r=[xres, rr], w=[r_y[ti]])
                    op("pool", lambda e: e.tensor_tensor(out=y[:, ti, :], in0=y[:, ti, :], in1=gB[:], op=ALU.mult), r=[r_gb, r_y[ti]], w=[r_y[ti]])
                    op("pool", lambda e: e.tensor_tensor(out=y[:, ti, :], in0=y[:, ti, :], in1=bB[:], op=ALU.add), r=[r_gb, r_y[ti]], w=[r_y[ti]])
                wout_v = wout_d.rearrange("(kc p) n -> p kc n", p=128)
                PFO = Prefetch(WO, [wout_v[:, :, cb * 512:(cb + 1) * 512] for cb in range(4)], 1)
                for cb in range(4):
                    wt, wr = PFO.get()
                    for ti in range(NT):
                        b = (cb * NT + ti) % 4
                        for kc in range(KC):
                            op("pe", lambda e, kc=kc: e.matmul(banks[b][:, 0:512], lhsT=mixT[:, kc, ti * 128:(ti + 1) * 128], rhs=wt[:, kc, :], start=(kc == 0), stop=(kc == KC - 1)), r=[wr], w=[bres[b]])
                        op("dve", lambda e: e.tensor_tensor(out=y[:, ti, cb * 512:(cb + 1) * 512], in0=banks[b][:, 0:512], in1=y[:, ti, cb * 512:(cb + 1) * 512], op=ALU.add), r=[bres[b], r_y[ti]], w=[r_y[ti]])
                S.barrier()

            with contextlib.ExitStack() as st5:
                h1T = mixT; r_h1T = [Res() for _ in range(NT)]
                hbr = Ring(S, st5, nc, "hb", 2, [128, D], BF16)
                dma("sp", "gbld", gB[:], l1g_d.partition_broadcast(128), w=[r_gb])
                dma("sp", "gbld", bB[:], l1b_d.partition_broadcast(128), w=[r_gb])
                for ti in range(NT):
                    mv, rstd, rr = ln_stats(lnt[ti % 2], y[:, ti, :], 128, r_y[ti])
                    op("dve", lambda e: e.tensor_scalar(out=y[:, ti, :], in0=y[:, ti, :], scalar1=mv[:, 0:1], scalar2=rstd[:, 0:1], op0=ALU.subtract, op1=ALU.mult), r=[rr, r_y[ti]], w=[r_y[ti]])
                    op("pool", lambda e: e.tensor_tensor(out=y[:, ti, :], in0=y[:, ti, :], in1=gB[:], op=ALU.mult), r=[r_gb, r_y[ti]], w=[r_y[ti]])
                    op("pool", lambda e: e.tensor_tensor(out=y[:, ti, :], in0=y[:, ti, :], in1=bB[:], op=ALU.add), r=[r_gb, r_y[ti]], w=[r_y[ti]])
                    hb, hbres, _ = hbr.next()
                    op("act", lambda e: e.copy(out=hb[:], in_=y[:, ti, :]), r=[r_y[ti]], w=[hbres])
                    for g4 in range(4):
                        b = g4 % 2
                        for j in range(4):
                            dc = g4 * 4 + j
                            op("pe", transpose_to(banks[b][:, j * 128:(j + 1) * 128], hb[:, dc * 128:(dc + 1) * 128], 128), r=[hbres, R_const], w=[bres[b]])
                        dst = h1T[:, g4 * 4:(g4 + 1) * 4, ti * 128:(ti + 1) * 128]
                        srcp = banks[b][:, 0:512].rearrange("p (j t) -> p j t", t=128)
                        evac_copy(g4, dst, srcp, [bres[b]], [r_h1T[ti]])
                    op("pool", lambda e: e.tensor_scalar(out=y[:, ti, :], in0=y[:, ti, :], scalar1=float(ALPHA), scalar2=None, op0=ALU.mult), r=[r_y[ti]], w=[r_y[ti]])
                WG = WLoader(S, st5, nc, "wgu", KC, 128, 2, 4)
                WD = WLoader(S, st5, nc, "wd", FG, 512, 2, 3)
                aT = [sb(st5, f"aT{i}", [128, FG, T], BF16) for i in range(2)]
                r_aT = [Res() for _ in range(2)]
                sg = [sb(st5, f"sg{i}", [128, HALF], BF16) for i in range(2)]
                r_sg = [Res() for _ in range(2)]
                wg_v = wg_d.rearrange("(kc p) n -> p kc n", p=128)
                wu_v = wu_d.rearrange("(kc p) n -> p kc n", p=128)
                wd_v = wd_d.rearrange("(fc p) n -> p fc n", p=128)
                k_sg = 0
                gspecs = []
                for fc in range(FC):
                    gspecs += [wg_v[:, :, fc * 128:(fc + 1) * 128], wu_v[:, :, fc * 128:(fc + 1) * 128]]
                PFG = Prefetch(WG, gspecs, 2)
                PFD = Prefetch(WD, [wd_v[:, g * FG:(g + 1) * FG, cb * 512:(cb + 1) * 512] for g in range(NG) for cb in range(4)], 1)
                for g in range(NG):
                    at, atr = aT[g % 2], r_aT[g % 2]
                    for f in range(FG):
                        fc = g * FG + f
                        wgt, wgr = PFG.get()
                        wut, wur = PFG.get()
                        for hf in range(NHALF):
                            bg_, bu_ = 0 + (hf % 2), 2 + (hf % 2)
                            tiles = list(range(hf * HALF // 128, (hf + 1) * HALF // 128))
                            for kc in range(KC):
                                op("pe", lambda e, kc=kc: e.matmul(banks[bg_][:, 0:HALF], lhsT=wgt[:, kc, :], rhs=h1T[:, kc, hf * HALF:(hf + 1) * HALF], start=(kc == 0), stop=(kc == KC - 1)), r=[wgr] + [r_h1T[t] for t in tiles], w=[bres[bg_]])
                            for kc in range(KC):
                                op("pe", lambda e, kc=kc: e.matmul(banks[bu_][:, 0:HALF], lhsT=wut[:, kc, :], rhs=h1T[:, kc, hf * HALF:(hf + 1) * HALF], start=(kc == 0), stop=(kc == KC - 1)), r=[wur] + [r_h1T[t] for t in tiles], w=[bres[bu_]])
                            s_, sr_ = sg[k_sg % 2], r_sg[k_sg % 2]
                            k_sg += 1
                            op("act", lambda e: e.activation(out=s_[:, 0:HALF], in_=banks[bg_][:, 0:HALF], func=AF.Silu), r=[bres[bg_]], w=[sr_])
                            op("dve", lambda e: e.tensor_tensor(out=at[:, f, hf * HALF:(hf + 1) * HALF], in0=banks[bu_][:, 0:HALF], in1=s_[:, 0:HALF], op=ALU.mult), r=[bres[bu_], sr_], w=[atr])
                    for cb in range(4):
                        wt, wr = PFD.get()
                        for ti in range(NT):
                            b = 4 + (cb * NT + ti) % 4
                            for f in range(FG):
                                op("pe", lambda e, f=f: e.matmul(banks[b][:, 0:512], lhsT=at[:, f, ti * 128:(ti + 1) * 128], rhs=wt[:, f, :], start=(f == 0), stop=(f == FG - 1)), r=[wr, atr], w=[bres[b]])
                            op("dve", lambda e: e.tensor_tensor(out=y[:, ti, cb * 512:(cb + 1) * 512], in0=banks[b][:, 0:512], in1=y[:, ti, cb * 512:(cb + 1) * 512], op=ALU.add), r=[bres[b], r_y[ti]], w=[r_y[ti]])
                dma("sp", "gbld", gB[:], l2g_d.partition_broadcast(128), r=[r_gb], w=[r_gb])
                dma("sp", "gbld", bB[:], l2b_d.partition_broadcast(128), r=[r_gb], w=[r_gb])
                for ti in range(NT):
                    mv, rstd, rr = ln_stats(lnt[ti % 2], y[:, ti, :], 128, r_y[ti])
                    op("dve", lambda e: e.tensor_scalar(out=y[:, ti, :], in0=y[:, ti, :], scalar1=mv[:, 0:1], scalar2=rstd[:, 0:1], op0=ALU.subtract, op1=ALU.mult), r=[rr, r_y[ti]], w=[r_y[ti]])
                    op("pool", lambda e: e.tensor_tensor(out=y[:, ti, :], in0=y[:, ti, :], in1=gB[:], op=ALU.mult), r=[r_gb, r_y[ti]], w=[r_y[ti]])
                    op("dve", lambda e: e.tensor_tensor(out=y[:, ti, :], in0=y[:, ti, :], in1=bB[:], op=ALU.add), r=[r_gb, r_y[ti]], w=[r_y[ti]])
                    dma("sp", "st", out_d[ti * 128:(ti + 1) * 128, :], y[:, ti, :], r=[r_y[ti]], w=[R_out])
                S.barrier()
        if ms_out is not None:
            ms_out.update(S.ms_rec)
    return nc


def build2(T, DFF, NCORES, DBG=False):
    ms = {}
    build(T, DFF, NCORES, DBG=DBG, ms_out=ms)
    return build(T, DFF, NCORES, DBG=DBG, milestones=ms)


def make_tables(c, T, NCORES):
    NT = T // 128
    TT = NM + T
    half = 128
    inv_freq = (np.float32(10000.0) ** (-np.arange(half, dtype=np.float32) / np.float32(half))).astype(np.float32)
    pos = np.concatenate([np.arange(NM), NM + c * T + np.arange(T)]).astype(np.float32)
    ang = (pos[None, :] * inv_freq[:, None]).astype(np.float32).astype(np.float64)
    cs = np.stack([np.cos(ang), np.sin(ang)], axis=1).astype(np.float32)
    g = np.array(GAMMAS, dtype=np.float64)
    s = np.arange(128, dtype=np.float64)
    rmask = np.zeros((128, RH, 128), np.float64)
    for h in range(RH):
        m = (s[None, :] >= s[:, None]).astype(np.float64) * (g[h] ** (-(s[:, None] + 1.0))) / 16.0
        rmask[:, h, :] = m
    kdec = (g[None, :] ** (127.0 - s[:, None])) / 16.0
    i_glob = (np.arange(NT)[None, :, None] * 128 + s[:, None, None])
    kdecf = (g[None, None, :] ** (T - 1.0 - i_glob)) / 16.0
    jm = np.arange(NM, dtype=np.float64)
    kdecm = (g[None, :] ** (NM - 1.0 - jm[:, None])) / 16.0
    epsr = HN_EPS * (g[None, :] ** (-2.0 * (s[:, None] + 1.0)))
    s64 = np.arange(64)
    hmask = (s64[None, :] >= s64[:, None]).astype(np.float32)
    rst = np.ones((128, TT), np.float32)
    rst[:, 0] = 0.0
    rst[:, NM::64] = 0.0
    sel = np.zeros((128, NCORES), np.float32)
    sel[:, c] = 1.0
    f32 = lambda a: np.ascontiguousarray(a, dtype=np.float32)
    return dict(cs=f32(cs), rmask=f32(rmask), kdec=f32(kdec), kdecf=f32(kdecf), kdecm=f32(kdecm),
                epsr=f32(epsr), hmask=f32(hmask), rst=f32(rst), sel=f32(sel),
                ident=np.eye(128, dtype=np.float32))


def make_in_maps(inp, T, NCORES):
    f = lambda a: np.ascontiguousarray(np.asarray(a, dtype=np.float32))
    x = f(inp["x"])[0]
    shared = dict(
        meta=f(inp["meta_tokens"]), ln_in_g=f(inp["ln_in_g"]), ln_in_b=f(inp["ln_in_b"]),
        lb2=f(inp["hg_lower_bounds"]), w_in=f(inp["w_in"])[0], hg_norm_g=f(inp["hg_norm_g"])[0],
        ret_norm_g=f(inp["ret_norm_g"])[0], ret_norm_b=f(inp["ret_norm_b"])[0], w_out=f(inp["w_out"])[0],
        ln1_g=f(inp["ln1_g"])[0], ln1_b=f(inp["ln1_b"])[0], w_gate=f(inp["w_gate"])[0],
        w_up=f(inp["w_up"])[0], w_down=f(inp["w_down"])[0], ln2_g=f(inp["ln2_g"])[0], ln2_b=f(inp["ln2_b"])[0])
    maps = []
    for c in range(NCORES):
        m = dict(shared)
        m["x"] = np.ascontiguousarray(x[c * T:(c + 1) * T])
        m.update(make_tables(c, T, NCORES))
        maps.append(m)
    return maps


def kernel(**inputs):
    NCORES = 8
    seq = inputs["x"].shape[1]
    T = seq // NCORES
    DFF = inputs["w_gate"].shape[2]
    nc = build2(T, DFF, NCORES)
    in_maps = make_in_maps(inputs, T, NCORES)
    res = run_bass_kernel_spmd(nc, in_maps, core_ids=list(range(NCORES)))
    out = np.concatenate([np.asarray(r["out"], dtype=np.float32) for r in res.results], axis=0)
    return out[None]
```

```python
import contextlib
import numpy as np
import concourse.bass as bass
import concourse.mybir as mybir
from concourse.bass_utils import run_bass_kernel_spmd

F32 = mybir.dt.float32
BF16 = mybir.dt.bfloat16
AF = mybir.ActivationFunctionType
ALU = mybir.AluOpType
AX = mybir.AxisListType

D = 2048
KC = D // 128
NM = 16
HGH = 8
RH = 4
NPROJ = 8192
LN_EPS = 1e-5
HN_EPS = 1e-6
ALPHA = 2.0 ** 0.25
GAMMAS = [1.0 - 2.0 ** (-5.0 - h) for h in range(RH)]


class Res:
    __slots__ = ("name", "w", "r")

    def __init__(self, name=""):
        self.name = name
        self.w = None
        self.r = []


class Sched:
    def __init__(self, nc, stack, milestones=None):
        self.nc = nc
        self.ms_rec = {}
        self.ms_idx = None
        if milestones is not None:
            self.ms_idx = {k: {c: i + 1 for i, c in enumerate(sorted(v))} for k, v in milestones.items()}
        self.eng = {"pe": nc.tensor, "act": nc.scalar, "dve": nc.vector,
                    "pool": nc.gpsimd, "sp": nc.sync}
        self.sems = {}
        self.cnt = {}
        self.seen = {k: {} for k in self.eng}
        self.stack = stack
        for k in ("pe", "act", "dve", "pool"):
            self.new_sem(k)

    def new_sem(self, key):
        s = self.stack.enter_context(self.nc.semaphore("s_" + key))
        self.sems[key] = s
        self.cnt[key] = 0
        return key

    def _waits(self, e, deps):
        need = {}
        for d in deps:
            if d is None:
                continue
            k, v = d
            if e == "pe" and k == "pe":
                continue
            if k not in self.eng:
                v = self.cnt[k]
            if v > need.get(k, 0):
                need[k] = v
        seen = self.seen[e]
        for k, v in need.items():
            if seen.get(k, 0) >= v:
                continue
            if k in self.eng:
                self.ms_rec.setdefault(k, set()).add(v)
                vv = self.ms_idx[k][v] if self.ms_idx is not None else v
            else:
                vv = v
            self.eng[e].wait_ge(self.sems[k], vv)
            seen[k] = v

    @staticmethod
    def _collect(r, w):
        deps = []
        for x in r:
            deps.append(x.w)
        for x in w:
            deps.append(x.w)
            deps.extend(x.r)
        return deps

    @staticmethod
    def _commit(dep, r, w):
        for x in r:
            x.r.append(dep)
        for x in w:
            x.w = dep
            x.r = []

    def op(self, e, fn, r=(), w=()):
        self._waits(e, self._collect(r, w))
        ins = fn(self.eng[e])
        self.cnt[e] += 1
        if self.ms_idx is None or self.cnt[e] in self.ms_idx.get(e, ()):
            ins.then_inc(self.sems[e], 1)
        self._commit((e, self.cnt[e]), r, w)

    def dma(self, q, semkey, out, in_, r=(), w=(), **kw):
        self._waits(q, self._collect(r, w))
        ins = self.eng[q].dma_start(out=out, in_=in_, **kw)
        self.cnt[semkey] += 16
        ins.then_inc(self.sems[semkey], 16)
        self._commit((semkey, self.cnt[semkey]), r, w)

    def barrier(self, engines=("pe", "act", "dve", "pool", "sp")):
        for e in engines:
            self._waits(e, [(k, v) for k, v in self.cnt.items() if v > 0 and k != e])


class Ring:
    def __init__(self, S, st, nc, name, n, shape, dt, dma=False):
        self.S = S
        self.tiles = [st.enter_context(nc.sbuf_tensor(f"sb_{name}{i}", shape, dt)) for i in range(n)]
        self.res = [Res(f"{name}{i}") for i in range(n)]
        self.sem = [S.new_sem(f"{name}{i}") for i in range(n)] if dma else None
        self.i = 0

    def next(self):
        i = self.i % len(self.tiles)
        self.i += 1
        return self.tiles[i], self.res[i], (self.sem[i] if self.sem else None)


class WLoader:
    def __init__(self, S, st, nc, name, kc, ncol, n_stage, n_bf, kc_stage=None):
        self.S = S
        self.kc = kc
        self.ncol = ncol
        self.kcs = kc_stage or kc
        self.stage = Ring(S, st, nc, name + "_st", n_stage, [128, self.kcs, ncol], F32, dma=True)
        self.bf = Ring(S, st, nc, name + "_bf", n_bf, [128, kc, ncol], BF16)
        self.k = 0

    def load(self, src):
        S = self.S
        bt, br, _ = self.bf.next()
        for k0 in range(0, self.kc, self.kcs):
            stt, sr, ssem = self.stage.next()
            S.dma("sp", ssem, stt[:], src[:, k0:k0 + self.kcs, :], w=[sr])
            e = ("pool", "act", "dve")[self.k % 3]
            self.k += 1
            if e == "act":
                S.op(e, lambda g, a=bt[:, k0:k0 + self.kcs, :], b=stt[:]: g.copy(out=a, in_=b), r=[sr], w=[br])
            else:
                S.op(e, lambda g, a=bt[:, k0:k0 + self.kcs, :], b=stt[:]: g.tensor_copy(out=a, in_=b), r=[sr], w=[br])
        return bt, br


class Prefetch:
    def __init__(self, loader, srcs, depth):
        self.L = loader
        self.srcs = srcs
        self.depth = depth
        self.loaded = []
        self.i = 0
        for _ in range(min(depth, len(srcs))):
            self.loaded.append(self.L.load(self.srcs[len(self.loaded)]))

    def get(self):
        if len(self.loaded) < len(self.srcs):
            self.loaded.append(self.L.load(self.srcs[len(self.loaded)]))
        t = self.loaded[self.i]
        self.i += 1
        return t


def build(T, DFF, NCORES, DBG=False, milestones=None, ms_out=None):
    NT = T // 128
    NCH = T // 64
    TT = NM + T
    FC = DFF // 128
    FG = 4
    NG = FC // FG
    assert FC % FG == 0
    HALF = min(512, T)
    NHALF = T // HALF
    W_AG = 1024 + 8 + 2048
    SOFF, DOFF, ROFF = 0, 1024, 1032

    nc = bass.Bass("TRN2", target_bir_lowering=False)
    di = lambda n, sh: nc.dram_tensor(n, sh, F32, kind="ExternalInput").ap()
    x_d = di("x", [T, D])
    meta_d = di("meta", [NM, D])
    lng_d = di("ln_in_g", [D]); lnb_d = di("ln_in_b", [D])
    lb_d = di("lb2", [2, 1024])
    win_d = di("w_in", [D, NPROJ])
    hgn_d = di("hg_norm_g", [1024])
    rng_d = di("ret_norm_g", [1024]); rnb_d = di("ret_norm_b", [1024])
    wout_d = di("w_out", [D, D])
    l1g_d = di("ln1_g", [D]); l1b_d = di("ln1_b", [D])
    wg_d = di("w_gate", [D, DFF]); wu_d = di("w_up", [D, DFF]); wd_d = di("w_down", [DFF, D])
    l2g_d = di("ln2_g", [D]); l2b_d = di("ln2_b", [D])
    cs_d = di("cs", [128, 2, TT])
    rmask_d = di("rmask", [128, RH, 128])
    kdec_d = di("kdec", [128, RH])
    kdecf_d = di("kdecf", [128, NT, RH])
    kdecm_d = di("kdecm", [NM, RH])
    epsr_d = di("epsr", [128, RH])
    hmask_d = di("hmask", [64, 64])
    rst_d = di("rst", [128, TT])
    sel_d = di("sel", [128, NCORES])
    ident_d = di("ident", [128, 128])
    out_d = nc.dram_tensor("out", [T, D], F32, kind="ExternalOutput").ap()
    if DBG:
        dbg_mix = nc.dram_tensor("dbg_mix", [128, KC, T], BF16, kind="ExternalOutput").ap()
        dbg_start = nc.dram_tensor("dbg_start", [128, 3072], F32, kind="ExternalOutput").ap()
        dbg_h0T = nc.dram_tensor("dbg_h0T", [128, KC, TT], BF16, kind="ExternalOutput").ap()
    ag_in = nc.dram_tensor("ag_in", [128, W_AG], F32)
    ag_out = nc.dram_tensor("ag_out", [NCORES * 128, W_AG], F32)

    win_v = win_d.rearrange("(kc p) n -> p kc n", p=128)

    with contextlib.ExitStack() as st0:
        S = Sched(nc, st0, milestones)
        op, dma = S.op, S.dma

        def sb(stk, name, shape, dt=F32):
            return stk.enter_context(nc.sbuf_tensor("sb_" + name, shape, dt))

        S.new_sem("ld")
        S.new_sem("st")
        S.new_sem("cc")
        ident_f = sb(st0, "ident_f", [128, 128]); ident = sb(st0, "ident", [128, 128], BF16)
        mixT = sb(st0, "mixT", [128, KC, T], BF16)
        stm = contextlib.ExitStack()
        st0_real = st0
        st0 = stm
        h0T = sb(st0, "h0T", [128, KC, TT], BF16)
        gin = sb(st0, "gin", [128, KC]); bin_ = sb(st0, "bin", [128, KC])
        lb = sb(st0, "lb", [128, HGH]); oml = sb(st0, "oml", [128, HGH]); lbt = sb(st0, "lbt", [128, 2, HGH])
        hgn = sb(st0, "hgn", [128, HGH]); rng_ = sb(st0, "rng", [128, 2 * RH]); rnb = sb(st0, "rnb", [128, 2 * RH])
        cs = sb(st0, "cs", [128, 2, TT])
        rmask = sb(st0, "rmask", [128, RH, 128]); kdec = sb(st0, "kdec", [128, RH])
        kdecf = sb(st0, "kdecf", [128, NT, RH]); kdecm = sb(st0, "kdecm", [NM, RH])
        epsr = sb(st0, "epsr", [128, RH]); hmask = sb(st0, "hmask", [64, 64])
        rst = sb(st0, "rst", [128, TT]); sel = sb(st0, "sel", [128, NCORES])
        sstart = sb(st0, "sstart", [128, 1024]); rstart = sb(st0, "rstart", [128, 2048])
        st0 = st0_real
        banks = [st0.enter_context(nc.psum_tensor(f"bank{i}", [128, 512], F32)) for i in range(8)]
        bres = [Res(f"bank{i}") for i in range(8)]
        R_const = Res("const")
        R_h0T = [Res(f"h0T{i}") for i in range(NT + 1)]
        R_mix = [Res(f"mix{i}") for i in range(KC)]
        R_ag = Res("agbuf"); R_smeta = Res("smeta"); R_rmeta = Res("rmeta")
        R_start = Res("start")
        R_agin = Res("agin"); R_agout = Res("agout"); R_out = Res("out")

        def ld(dst, src, **kw):
            dma("sp", "ld", dst, src, w=[R_const], **kw)

        ld(ident_f[:], ident_d)
        ld(gin[:], lng_d.rearrange("(kc p) -> p kc", p=128), allow_slow_non_contiguous=True)
        ld(bin_[:], lnb_d.rearrange("(kc p) -> p kc", p=128), allow_slow_non_contiguous=True)
        ld(lbt[:], lb_d.rearrange("a (h p) -> p a h", p=128), allow_slow_non_contiguous=True)
        ld(hgn[:], hgn_d.rearrange("(h p) -> p h", p=128), allow_slow_non_contiguous=True)
        ld(rng_[:], rng_d.rearrange("(h p) -> p h", p=128), allow_slow_non_contiguous=True)
        ld(rnb[:], rnb_d.rearrange("(h p) -> p h", p=128), allow_slow_non_contiguous=True)
        ld(cs[:], cs_d); ld(rmask[:], rmask_d); ld(kdec[:], kdec_d); ld(kdecf[:], kdecf_d)
        ld(kdecm[:], kdecm_d); ld(epsr[:], epsr_d); ld(hmask[:], hmask_d); ld(rst[:], rst_d); ld(sel[:], sel_d)
        op("dve", lambda e: e.tensor_copy(out=ident[:], in_=ident_f[:]), r=[R_const], w=[R_const])
        op("dve", lambda e: e.tensor_tensor(out=lb[:], in0=lbt[:, 0, :], in1=lbt[:, 1, :], op=ALU.subtract), r=[R_const], w=[R_const])
        op("act", lambda e: e.activation(out=lb[:], in_=lb[:], func=AF.Sigmoid), r=[R_const], w=[R_const])
        op("dve", lambda e: e.tensor_scalar(out=oml[:], in0=lb[:], scalar1=-1.0, scalar2=1.0, op0=ALU.mult, op1=ALU.add), r=[R_const], w=[R_const])
        S.barrier()

        def transpose_to(ps_ap, src_ap, k):
            return lambda e: e.matmul(ps_ap, lhsT=src_ap, rhs=ident[0:k, 0:k], start=True, stop=True)

        def ln_stats(stk_tiles, src, rows, rsrc, eps_bias=LN_EPS):
            stt, mv, rstd, rr = stk_tiles
            for c in range(4):
                op("dve", lambda e, c=c: e.bn_stats(out=stt[:rows, c, :], in_=src[:rows, c * 512:(c + 1) * 512]), r=[rsrc], w=[rr])
            op("dve", lambda e: e.bn_aggr(out=mv[:rows, :], in_=stt[:rows].rearrange("p c s -> p (c s)")), r=[rr], w=[rr])
            op("act", lambda e: e.activation(out=rstd[:rows, 0:1], in_=mv[:rows, 1:2], func=AF.Ln, bias=eps_bias), r=[rr], w=[rr])
            op("act", lambda e: e.activation(out=rstd[:rows, 0:1], in_=rstd[:rows, 0:1], func=AF.Exp, scale=-0.5), r=[rr], w=[rr])
            return mv, rstd, rr

        def ln_apply(dst, src, mv, rstd, rr, rsrc, rdst, gB_, bB_, r_gb_):
            op("dve", lambda e: e.scalar_tensor_tensor(out=rstd[:, 1:2], in0=mv[:, 0:1], scalar=-1.0, in1=rstd[:, 0:1], op0=ALU.mult, op1=ALU.mult), r=[rr], w=[rr])
            op("act", lambda e: e.activation(out=dst, in_=src, func=AF.Identity, scale=rstd[:, 0:1], bias=rstd[:, 1:2]), r=[rr, rsrc], w=[rdst])
            op("dve", lambda e: e.tensor_tensor(out=dst, in0=dst, in1=gB_[:], op=ALU.mult), r=[r_gb_, rdst], w=[rdst])
            op("dve", lambda e: e.tensor_tensor(out=dst, in0=dst, in1=bB_[:], op=ALU.add), r=[r_gb_, rdst], w=[rdst])

        with contextlib.ExitStack() as st:
            xr = Ring(S, st, nc, "xt", 2, [128, D], F32, dma=True)
            xhr = Ring(S, st, nc, "xh", 2, [128, D], BF16)
            lnt = [(sb(st, f"lnst{i}", [128, 4, 6]), sb(st, f"lnmv{i}", [128, 2]), sb(st, f"lnrs{i}", [128, 2]), Res()) for i in range(2)]
            for ti in range(NT + 1):
                rows = NM if ti == 0 else 128
                col0 = 0 if ti == 0 else NM + (ti - 1) * 128
                src = meta_d if ti == 0 else x_d[(ti - 1) * 128: ti * 128, :]
                xt, xres, xsem = xr.next()
                dma("sp", xsem, xt[:rows, :], src, w=[xres])
                mv, rstd, rr = ln_stats(lnt[ti % 2], xt, rows, xres)
                xh, xhres, _ = xhr.next()
                op("dve", lambda e: e.tensor_scalar(out=xh[:rows, :], in0=xt[:rows, :], scalar1=mv[:rows, 0:1], scalar2=rstd[:rows, 0:1], op0=ALU.subtract, op1=ALU.mult), r=[xres, rr], w=[xhres])
                for g4 in range(4):
                    b = g4 % 2
                    for j in range(4):
                        dc = g4 * 4 + j
                        op("pe", transpose_to(banks[b][:, j * 128: j * 128 + rows], xh[:rows, dc * 128:(dc + 1) * 128], rows), r=[xhres, R_const], w=[bres[b]])
                    for j in range(4):
                        dc = g4 * 4 + j
                        if b == 0:
                            op("act", lambda e, j=j, dc=dc: e.activation(out=h0T[:, dc, col0:col0 + rows], in_=banks[b][:, j * 128: j * 128 + rows], func=AF.Identity, scale=gin[:, dc:dc + 1], bias=bin_[:, dc:dc + 1]), r=[bres[b], R_const], w=[R_h0T[ti]])
                        else:
                            op("dve", lambda e, j=j, dc=dc: e.tensor_scalar(out=h0T[:, dc, col0:col0 + rows], in0=banks[b][:, j * 128: j * 128 + rows], scalar1=gin[:, dc:dc + 1], scalar2=bin_[:, dc:dc + 1], op0=ALU.mult, op1=ALU.add), r=[bres[b], R_const], w=[R_h0T[ti]])
            S.barrier()

        def h0res(tok0, n):
            out = []
            for ti in range(NT + 1):
                a = 0 if ti == 0 else NM + (ti - 1) * 128
                b_ = NM if ti == 0 else a + 128
                if a < tok0 + n and b_ > tok0:
                    out.append(R_h0T[ti])
            return out

        def proj(wt, wr, blocks):
            for (tok0, n, b) in blocks:
                for kc in range(KC):
                    op("pe", lambda e, kc=kc: e.matmul(banks[b][:, 0:n], lhsT=wt[:, kc, :], rhs=h0T[:, kc, tok0:tok0 + n], start=(kc == 0), stop=(kc == KC - 1)),
                       r=[wr] + h0res(tok0, n), w=[bres[b]])

        main_blocks = lambda b0: [(NM + i * HALF, HALF, b0 + i) for i in range(NHALF)]

        with contextlib.ExitStack() as st:
            WL = WLoader(S, st, nc, "win", KC, 128, 2, 5)
            wcol = lambda c0: win_v[:, :, c0:c0 + 128]
            specs = []
            for h in range(HGH):
                specs += [wcol(1024 + h * 128), wcol(2048 + h * 128)]
            for h in range(RH):
                specs += [wcol(5 * 1024 + h * 256), wcol(5 * 1024 + h * 256 + 128), wcol(6 * 1024 + h * 256), wcol(6 * 1024 + h * 256 + 128)]
            for h in range(HGH):
                specs += [wcol(1 * 1024 + h * 128), wcol(0 * 1024 + h * 128), wcol(2 * 1024 + h * 128), wcol(3 * 1024 + h * 128)]
            for h in range(RH):
                specs += [wcol(g * 1024 + h * 256 + a_ * 128) for g in (4, 5, 6, 7) for a_ in range(2)]
            PFW = None
            fm32 = [sb(st, f"fm32_{i}", [128, TT]) for i in range(5)]
            fmr = [Res() for _ in range(5)]
            fmb = [sb(st, f"fmb_{i}", [128, TT], BF16) for i in range(8)]
            fbr = [Res() for _ in range(8)]
            TMW = max((NCH + 1) * 128, (NT + 1) * 256)
            tmk_f = sb(st, "tmk", [128, TMW], BF16); r_tmk = Res()
            tmv_f = sb(st, "tmv", [128, TMW], BF16); r_tmv = Res()

            class _TM:
                def __init__(self, t):
                    self.t = t
                    self.w = 128

                def __getitem__(self, key):
                    p, ci, c = key
                    return self.t[p, ci * self.w + c.start: ci * self.w + c.stop]
            tmk = _TM(tmk_f); tmv = _TM(tmv_f)
            small = sb(st, "small", [128, 64]); r_small = Res()
            S32 = sb(st, "S32", [128, 512]); Sbf = sb(st, "Sbf", [128, 512], BF16); r_S = Res(); r_Sbf = Res()
            scm = sb(st, "scm", [128, 128], BF16); r_scm = Res()
            xhc = sb(st, "xhc", [128, 256], BF16); r_xhc = Res()
            junk = sb(st, "junk", [128, 256]); r_junk = Res()
            cst = sb(st, "cst", [128, 6]); cmv = sb(st, "cmv", [128, 2]); crs = sb(st, "crs", [128, 2]); r_c = Res()
            tmp32 = sb(st, "tmp32", [128, 128]); r_tmp = Res()
            S.new_sem("gb0")

            stp1 = contextlib.ExitStack()
            agbuf = sb(stp1, "agbuf", [128, W_AG])
            smeta = sb(stp1, "smeta", [128, 1024]); rmeta = sb(stp1, "rmeta", [128, 2048])
            if DBG:
                print("SBUF remaining in mixer stage:", nc.sbuf_bytes_remaining)

            def evac_copy(i, dst, src, r, w):
                if i % 2 == 0:
                    op("act", lambda e: e.copy(out=dst, in_=src), r=r, w=w)
                else:
                    op("dve", lambda e: e.tensor_copy(out=dst, in_=src), r=r, w=w)

            def hg_f_chain(h, blocks, ncols, c0, want_kt):
                sig, lgf, kk, cum = fm32[0], fm32[1], fm32[2], fm32[4]
                for (tok0, n, b) in blocks:
                    a = tok0 - c0
                    op("act", lambda e, a=a, n=n, b=b: e.activation(out=sig[:, a:a + n], in_=banks[b][:, 0:n], func=AF.Sigmoid), r=[bres[b]], w=[fmr[0]])
                    op("act", lambda e, a=a, n=n, b=b: e.activation(out=kk[:, a:a + n], in_=banks[b][:, 0:n], func=AF.Sigmoid, scale=-1.0), r=[bres[b]], w=[fmr[2]])
                op("act", lambda e: e.activation(out=lgf[:, 0:ncols], in_=sig[:, 0:ncols], func=AF.Ln, scale=oml[:, h:h + 1], bias=lb[:, h:h + 1]), r=[fmr[0], R_const], w=[fmr[1]])
                op("dve", lambda e: e.tensor_tensor_scan(out=cum[:, 0:ncols], data0=rst[:, c0:c0 + ncols], data1=lgf[:, 0:ncols], initial=0.0, op0=ALU.mult, op1=ALU.add), r=[fmr[1], R_const], w=[fmr[4]])
                chunks = []
                a = 0
                if c0 == 0:
                    chunks.append((0, NM)); a = NM
                while a < ncols:
                    chunks.append((a, 64)); a += 64
                nch = len(chunks)
                for ci, (a, n) in enumerate(chunks):
                    pass
                if c0 == 0:
                    op("dve", lambda e: e.tensor_copy(out=small[:, 0:1], in_=cum[:, NM - 1:NM]), r=[fmr[4]], w=[r_small])
                    mc0, k0 = NM, 1
                else:
                    mc0, k0 = 0, 0
                nmain = (ncols - mc0) // 64
                cm = cum[:, mc0:ncols].rearrange("p (c j) -> p c j", j=64)
                op("dve", lambda e: e.tensor_copy(out=small[:, k0:k0 + nmain], in_=cm[:, :, 63]), r=[fmr[4]], w=[r_small])
                op("act", lambda e: e.activation(out=small[:, 32:32 + nch], in_=small[:, 0:nch], func=AF.Exp), r=[r_small], w=[r_small])
                if c0 == 0:
                    op("dve", lambda e: e.tensor_scalar(out=sig[:, 0:NM], in0=cum[:, 0:NM], scalar1=small[:, 0:1], scalar2=None, op0=ALU.subtract), r=[fmr[4], r_small], w=[fmr[0]])
                sm = sig[:, mc0:ncols].rearrange("p (c j) -> p c j", j=64)
                op("dve", lambda e: e.tensor_tensor(out=sm, in0=cm, in1=small[:, k0:k0 + nmain].unsqueeze(2).to_broadcast([128, nmain, 64]), op=ALU.subtract), r=[fmr[4], r_small], w=[fmr[0]])
                op("act", lambda e: e.activation(out=sig[:, 0:ncols], in_=sig[:, 0:ncols], func=AF.Exp, scale=-1.0), r=[fmr[0]], w=[fmr[0]])
                op("dve", lambda e: e.scalar_tensor_tensor(out=fmb[0][:, 0:ncols], in0=kk[:, 0:ncols], scalar=oml[:, h:h + 1], in1=sig[:, 0:ncols], op0=ALU.mult, op1=ALU.mult), r=[fmr[2], fmr[0], R_const], w=[fbr[0]])
                if want_kt:
                    op("act", lambda e: e.activation(out=fm32[3][:, 0:ncols], in_=cum[:, 0:ncols], func=AF.Exp), r=[fmr[4]], w=[fmr[3]])
                    op("act", lambda e: e.activation(out=lgf[:, 0:ncols], in_=cum[:, 0:ncols], func=AF.Exp, scale=-1.0), r=[fmr[4]], w=[fmr[1]])
                    op("dve", lambda e: e.scalar_tensor_tensor(out=fmb[1][:, 0:ncols], in0=kk[:, 0:ncols], scalar=oml[:, h:h + 1], in1=lgf[:, 0:ncols], op0=ALU.mult, op1=ALU.mult), r=[fmr[2], fmr[1], R_const], w=[fbr[1]])
                return chunks

            def to_tm(srcs, chunks, dst, rdst, width):
                i = 0
                for ci, (a, n) in enumerate(chunks):
                    b = 4 + (ci % 2)
                    for j, (sap, sres) in enumerate(srcs):
                        op("pe", transpose_to(banks[b][0:n, j * 128:(j + 1) * 128], sap[:, a:a + n], 128), r=[sres, R_const], w=[bres[b]])
                    evac_copy(ci, dst[0:n, ci, 0:width], banks[b][0:n, 0:width], [bres[b]], [rdst])

            all_blocks = [(0, NM, 2)] + main_blocks(0)
            for h in range(HGH):
                if PFW is None:
                    PFW = Prefetch(WL, specs, 2)
                wf, wfr = PFW.get()
                proj(wf, wfr, all_blocks)
                chunks = hg_f_chain(h, all_blocks, TT, 0, False)
                wi, wir = PFW.get()
                proj(wi, wir, [(0, NM, 3)] + main_blocks(6))
                for (tok0, n, b) in [(0, NM, 3)] + main_blocks(6):
                    evac_copy(b, fmb[2][:, tok0:tok0 + n], banks[b][:, 0:n], [bres[b]], [fbr[2]])
                to_tm([(fmb[0], fbr[0])], chunks, tmk, r_tmk, 128)
                to_tm([(fmb[2], fbr[2])], chunks, tmv, r_tmv, 128)
                for ci, (a, n) in enumerate(chunks):
                    b = 4 + (ci % 2)
                    op("pe", lambda e, ci=ci, n=n, b=b: e.matmul(banks[b][:, 0:128], lhsT=tmk[0:n, ci, 0:128], rhs=tmv[0:n, ci, 0:128], start=True, stop=True), r=[r_tmk, r_tmv], w=[bres[b]])
                    if ci == 0:
                        op("dve", lambda e, b=b: e.tensor_copy(out=smeta[:, h * 128:(h + 1) * 128], in_=banks[b][:, 0:128]), r=[bres[b]], w=[R_smeta])
                    elif ci == 1:
                        op("dve", lambda e, b=b: e.tensor_copy(out=S32[:, 0:128], in_=banks[b][:, 0:128]), r=[bres[b]], w=[r_S])
                    else:
                        op("dve", lambda e, b=b, ci=ci: e.scalar_tensor_tensor(out=S32[:, 0:128], in0=S32[:, 0:128], scalar=small[:, 32 + ci:33 + ci], in1=banks[b][:, 0:128], op0=ALU.mult, op1=ALU.add), r=[bres[b], r_S, r_small], w=[r_S])
                op("act", lambda e: e.copy(out=agbuf[:, SOFF + h * 128: SOFF + (h + 1) * 128], in_=S32[:, 0:128]), r=[r_S], w=[R_ag])
                op("dve", lambda e: e.tensor_reduce(out=small[:, 30:31], in_=small[:, 1:1 + NCH], axis=AX.X, op=ALU.add), r=[r_small], w=[r_small])
                op("act", lambda e: e.activation(out=agbuf[:, DOFF + h:DOFF + h + 1], in_=small[:, 30:31], func=AF.Exp), r=[r_small], w=[R_ag])

            def rotary(bk, c0, ncols, d1, d2, blocks):
                for (tok0, n, b1), (_, _, b2) in zip(blocks[0], blocks[1]):
                    a = tok0 - c0
                    cosv = cs[:, 0, tok0:tok0 + n]; sinv = cs[:, 1, tok0:tok0 + n]
                    op("dve", lambda e: e.tensor_tensor(out=fm32[0][:, a:a + n], in0=banks[b1][:, 0:n], in1=cosv, op=ALU.mult), r=[bres[b1], R_const], w=[fmr[0]])
                    op("dve", lambda e: e.tensor_tensor(out=fm32[1][:, a:a + n], in0=banks[b2][:, 0:n], in1=sinv, op=ALU.mult), r=[bres[b2], R_const], w=[fmr[1]])
                    op("dve", lambda e: e.tensor_tensor(out=fm32[2][:, a:a + n], in0=banks[b1][:, 0:n], in1=sinv, op=ALU.mult), r=[bres[b1], R_const], w=[fmr[2]])
                    op("dve", lambda e: e.tensor_tensor(out=fm32[3][:, a:a + n], in0=banks[b2][:, 0:n], in1=cosv, op=ALU.mult), r=[bres[b2], R_const], w=[fmr[3]])
                op("pool", lambda e: e.tensor_tensor(out=fmb[d1][:, 0:ncols], in0=fm32[0][:, 0:ncols], in1=fm32[1][:, 0:ncols], op=ALU.subtract), r=[fmr[0], fmr[1]], w=[fbr[d1]])
                op("pool", lambda e: e.tensor_tensor(out=fmb[d2][:, 0:ncols], in0=fm32[2][:, 0:ncols], in1=fm32[3][:, 0:ncols], op=ALU.add), r=[fmr[2], fmr[3]], w=[fbr[d2]])

            rchunks = [(0, NM)] + [(NM + i * 128, 128) for i in range(NT)]
            tmk.w = tmv.w = 256
            for h in range(RH):
                blk1 = [(0, NM, 2)] + main_blocks(0)
                blk2 = [(0, NM, 3)] + main_blocks(6)
                wk1, wk1r = PFW.get()
                proj(wk1, wk1r, blk1)
                wk2, wk2r = PFW.get()
                proj(wk2, wk2r, blk2)
                rotary(None, 0, TT, 0, 1, (blk1, blk2))
                wv1, wv1r = PFW.get()
                proj(wv1, wv1r, blk1)
                for (tok0, n, b) in blk1:
                    evac_copy(b, fmb[2][:, tok0:tok0 + n], banks[b][:, 0:n], [bres[b]], [fbr[2]])
                wv2, wv2r = PFW.get()
                proj(wv2, wv2r, blk2)
                for (tok0, n, b) in blk2:
                    evac_copy(b + 1, fmb[3][:, tok0:tok0 + n], banks[b][:, 0:n], [bres[b]], [fbr[3]])
                for ci, (a, n) in enumerate(rchunks):
                    b = 4 + (ci % 2)
                    for j in range(2):
                        op("pe", transpose_to(banks[b][0:n, j * 128:(j + 1) * 128], fmb[j][:, a:a + n], 128), r=[fbr[j], R_const], w=[bres[b]])
                    sc_ap = kdecm[0:n, h:h + 1] if ci == 0 else kdecf[:, ci - 1, h:h + 1]
                    op("dve", lambda e, b=b, n=n, ci=ci, sc_ap=sc_ap: e.tensor_scalar(out=tmk[0:n, ci, 0:256], in0=banks[b][0:n, 0:256], scalar1=sc_ap, scalar2=None, op0=ALU.mult), r=[bres[b], R_const], w=[r_tmk])
                to_tm([(fmb[2], fbr[2]), (fmb[3], fbr[3])], rchunks, tmv, r_tmv, 256)
                for a_ in range(2):
                    op("pe", lambda e, a_=a_: e.matmul(banks[4][:, a_ * 256:(a_ + 1) * 256], lhsT=tmk[0:NM, 0, a_ * 128:(a_ + 1) * 128], rhs=tmv[0:NM, 0, 0:256], start=True, stop=True), r=[r_tmk, r_tmv], w=[bres[4]])
                op("act", lambda e: e.copy(out=rmeta[:, h * 512:(h + 1) * 512], in_=banks[4][:, 0:512]), r=[bres[4]], w=[R_rmeta])
                for a_ in range(2):
                    b = 5 + a_
                    for ci in range(1, NT + 1):
                        op("pe", lambda e, a_=a_, ci=ci, b=b: e.matmul(banks[b][:, 0:256], lhsT=tmk[:, ci, a_ * 128:(a_ + 1) * 128], rhs=tmv[:, ci, 0:256], start=(ci == 1), stop=(ci == NT)), r=[r_tmk, r_tmv], w=[bres[b]])
                    evac_copy(a_, agbuf[:, ROFF + h * 512 + a_ * 256: ROFF + h * 512 + (a_ + 1) * 256], banks[b][:, 0:256], [bres[b]], [R_ag])

            dma("sp", "st", ag_in.ap(), agbuf[:], r=[R_ag], w=[R_agin])
            S._waits("pool", [R_agin.w])
            ins = nc.gpsimd.collective_compute("AllGather", ALU.bypass, replica_groups=[list(range(NCORES))], ins=[ag_in.ap()], outs=[ag_out.ap()])
            ins.then_inc(S.sems["cc"], 1)
            S.cnt["cc"] += 1
            R_agout.w = ("cc", S.cnt["cc"])
            op("dve", lambda e: e.tensor_scalar(out=sstart[:], in0=smeta[:], scalar1=sel[:, 0:1], scalar2=None, op0=ALU.mult), r=[R_smeta, R_const], w=[R_start])
            op("pool", lambda e: e.tensor_scalar(out=rstart[:], in0=rmeta[:], scalar1=sel[:, 0:1], scalar2=None, op0=ALU.mult), r=[R_rmeta, R_const], w=[R_start])
            for j in range(NCORES - 1):
                gb, gr, gs = agbuf, R_ag, "gb0"
                dma("sp", gs, gb[:], ag_out.ap()[j * 128:(j + 1) * 128, :], r=[R_agout], w=[gr])
                sm3 = smeta[:].rearrange("p (h v) -> p h v", v=128)
                op("dve", lambda e, gb=gb: e.tensor_tensor(out=sm3, in0=sm3, in1=gb[:, DOFF:DOFF + 8].unsqueeze(2).to_broadcast([128, 8, 128]), op=ALU.mult), r=[gr, R_smeta], w=[R_smeta])
                op("dve", lambda e, gb=gb: e.tensor_tensor(out=smeta[:], in0=smeta[:], in1=gb[:, SOFF:SOFF + 1024], op=ALU.add), r=[gr, R_smeta], w=[R_smeta])
                op("dve", lambda e, j=j: e.scalar_tensor_tensor(out=sstart[:], in0=smeta[:], scalar=sel[:, j + 1:j + 2], in1=sstart[:], op0=ALU.mult, op1=ALU.add), r=[R_smeta, R_const, R_start], w=[R_start])
                for h in range(RH):
                    op("dve", lambda e, gb=gb, h=h: e.scalar_tensor_tensor(out=rmeta[:, h * 512:(h + 1) * 512], in0=rmeta[:, h * 512:(h + 1) * 512], scalar=float(GAMMAS[h] ** T), in1=gb[:, ROFF + h * 512: ROFF + (h + 1) * 512], op0=ALU.mult, op1=ALU.add), r=[gr, R_rmeta], w=[R_rmeta])
                op("dve", lambda e, j=j: e.scalar_tensor_tensor(out=rstart[:], in0=rmeta[:], scalar=sel[:, j + 1:j + 2], in1=rstart[:], op0=ALU.mult, op1=ALU.add), r=[R_rmeta, R_const, R_start], w=[R_start])

            stp2 = contextlib.ExitStack()
            crsA = sb(stp2, "crsA", [128, 16])
            junkA = fm32[0]
            S32p = [S32, fm32[1]]; r_S2 = [r_S, fmr[1]]
            xhcA = fmb[5]; scmA = fmb[6]

            SPB = TT // 128

            def sall(ci):
                t = fmb[7] if ci < SPB else fmb[0]
                return t[:, (ci % SPB) * 128:(ci % SPB + 1) * 128]

            def r_sall(ci):
                return fbr[7] if ci < SPB else fbr[0]
            mb = main_blocks(0)
            hchunks = [(i * 64, 64) for i in range(NCH)]
            tmk.w = tmv.w = 128
            for h in range(HGH):
                wf, wfr = PFW.get()
                proj(wf, wfr, mb)
                hg_f_chain(h, mb, T, NM, True)
                mb2 = main_blocks(6)
                wq, wqr = PFW.get()
                proj(wq, wqr, mb2)
                for (tok0, n, b) in mb2:
                    a = tok0 - NM
                    op("act", lambda e, a=a, n=n, b=b: e.activation(out=fm32[0][:, a:a + n], in_=banks[b][:, 0:n], func=AF.Silu), r=[bres[b]], w=[fmr[0]])
                op("dve", lambda e: e.scalar_tensor_tensor(out=fmb[3][:, 0:T], in0=fm32[0][:, 0:T], scalar=float(128.0 ** -0.5), in1=fm32[3][:, 0:T], op0=ALU.mult, op1=ALU.mult), r=[fmr[0], fmr[3]], w=[fbr[3]])
                wi, wir = PFW.get()
                proj(wi, wir, mb)
                for (tok0, n, b) in mb:
                    evac_copy(b, fmb[2][:, tok0 - NM:tok0 - NM + n], banks[b][:, 0:n], [bres[b]], [fbr[2]])
                wgt, wgr = PFW.get()
                proj(wgt, wgr, mb2)
                for (tok0, n, b) in mb2:
                    a = tok0 - NM
                    op("act", lambda e, a=a, n=n, b=b: e.activation(out=fmb[4][:, a:a + n], in_=banks[b][:, 0:n], func=AF.Silu), r=[bres[b]], w=[fbr[4]])
                to_tm([(fmb[0], fbr[0])], hchunks, tmk, r_tmk, 128)
                to_tm([(fmb[2], fbr[2])], hchunks, tmv, r_tmv, 128)
                kt, qt = fmb[1], fmb[3]
                GC = min(8, NCH, TT // 128)
                op("dve", lambda e: e.tensor_copy(out=S32p[0][:, 0:128], in_=sstart[:, h * 128:(h + 1) * 128]), r=[R_start], w=[r_S2[0]])
                op("act", lambda e: e.copy(out=sall(0), in_=sstart[:, h * 128:(h + 1) * 128]), r=[R_start], w=[r_sall(0)])
                for g0 in range(0, NCH - 1, 4):
                    nb_ = min(4, NCH - 1 - g0)
                    b = 4 if (g0 // 4) % 2 == 0 else 7
                    for i in range(nb_):
                        ci = g0 + i
                        op("pe", lambda e, ci=ci, i=i: e.matmul(banks[b][:, i * 128:(i + 1) * 128], lhsT=tmk[0:64, ci, 0:128], rhs=tmv[0:64, ci, 0:128], start=True, stop=True), r=[r_tmk, r_tmv], w=[bres[b]])
                    for i in range(nb_):
                        ci = g0 + i
                        src, dst = S32p[ci % 2], S32p[(ci + 1) % 2]
                        op("dve", lambda e, ci=ci, i=i: e.scalar_tensor_tensor(out=dst[:, 0:128], in0=src[:, 0:128], scalar=small[:, 32 + ci:33 + ci], in1=banks[b][:, i * 128:(i + 1) * 128], op0=ALU.mult, op1=ALU.add), r=[bres[b], r_S2[ci % 2], r_small], w=[r_S2[(ci + 1) % 2]])
                        op("pool", lambda e, ci=ci: e.tensor_copy(out=sall(ci + 1), in_=dst[:, 0:128]), r=[r_S2[(ci + 1) % 2]], w=[r_sall(ci + 1)])
                for g0 in range(0, NCH, GC):
                    for i in range(GC):
                        a = (g0 + i) * 64
                        op("pe", lambda e, a=a, i=i: e.matmul(banks[4][0:64, i * 64:(i + 1) * 64], lhsT=kt[:, a:a + 64], rhs=qt[:, a:a + 64], start=True, stop=True), r=[fbr[1], fbr[3]], w=[bres[4]])
                    op("dve", lambda e: e.tensor_tensor(out=scmA[0:64, 0:GC * 64].rearrange("p (c j) -> p c j", j=64), in0=banks[4][0:64, 0:GC * 64].rearrange("p (c j) -> p c j", j=64), in1=hmask[:].unsqueeze(1).to_broadcast([64, GC, 64]), op=ALU.mult), r=[bres[4], R_const], w=[fbr[6]])
                    for i in range(GC):
                        ci = g0 + i
                        a = ci * 64
                        b = 5 + i // 4
                        c0 = (i % 4) * 128
                        op("pe", lambda e, ci=ci, i=i, b=b, c0=c0: e.matmul(banks[b][0:64, c0:c0 + 128], lhsT=scmA[0:64, i * 64:(i + 1) * 64], rhs=tmv[0:64, ci, 0:128], start=True, stop=False), r=[fbr[6], r_tmv], w=[bres[b]])
                        op("pe", lambda e, ci=ci, a=a, b=b, c0=c0: e.matmul(banks[b][0:64, c0:c0 + 128], lhsT=qt[:, a:a + 64], rhs=sall(ci), start=False, stop=True), r=[fbr[3], r_sall(ci)], w=[bres[b]])
                    nbk = (GC + 3) // 4
                    for k in range(nbk):
                        w_ = min(4, GC - 4 * k) * 128
                        op("act", lambda e, k=k, w_=w_: e.activation(out=junkA[0:64, k * 512:k * 512 + w_], in_=banks[5 + k][0:64, 0:w_], func=AF.Square), r=[bres[5 + k]], w=[fmr[0]])
                    op("dve", lambda e: e.tensor_reduce(out=crsA[0:64, 0:GC], in_=junkA[0:64, 0:GC * 128].rearrange("p (c v) -> p c v", v=128), axis=AX.X, op=ALU.add), r=[fmr[0]], w=[r_c])
                    op("act", lambda e: e.activation(out=crsA[0:64, 8:8 + GC], in_=crsA[0:64, 0:GC], func=AF.Ln, scale=1.0 / 128.0, bias=HN_EPS), r=[r_c], w=[r_c])
                    op("act", lambda e: e.activation(out=crsA[0:64, 8:8 + GC], in_=crsA[0:64, 8:8 + GC], func=AF.Exp, scale=-0.5), r=[r_c], w=[r_c])
                    for k in range(nbk):
                        nck = min(4, GC - 4 * k)
                        w_ = nck * 128
                        op("dve", lambda e, k=k, w_=w_, nck=nck: e.tensor_tensor(out=xhcA[0:64, k * 512:k * 512 + w_].rearrange("p (c v) -> p c v", v=128), in0=banks[5 + k][0:64, 0:w_].rearrange("p (c v) -> p c v", v=128), in1=crsA[0:64, 8 + 4 * k:8 + 4 * k + nck].unsqueeze(2).to_broadcast([64, nck, 128]), op=ALU.mult), r=[bres[5 + k], r_c], w=[fbr[5]])
                    for i in range(GC):
                        op("pe", transpose_to(banks[7][:, i * 64:(i + 1) * 64], xhcA[0:64, i * 128:(i + 1) * 128], 64), r=[fbr[5], R_const], w=[bres[7]])
                    t0_, nt_ = g0 * 64, GC * 64
                    op("dve", lambda e: e.scalar_tensor_tensor(out=mixT[:, h, t0_:t0_ + nt_], in0=banks[7][:, 0:nt_], scalar=hgn[:, h:h + 1], in1=fmb[4][:, t0_:t0_ + nt_], op0=ALU.mult, op1=ALU.mult), r=[bres[7], fbr[4], R_const], w=[R_mix[h]])

            tchunks = [(i * 128, 128) for i in range(NT)]
            tmk.w = tmv.w = 256
            for h in range(RH):
                mb2 = main_blocks(6)
                blk = ([(t0, n, b) for (t0, n, b) in mb], [(t0, n, b) for (t0, n, b) in mb2])
                wq1 = PFW.get(); proj(wq1[0], wq1[1], mb)
                wq2 = PFW.get(); proj(wq2[0], wq2[1], mb2)
                rotary(None, NM, T, 0, 1, blk)
                wk1 = PFW.get(); proj(wk1[0], wk1[1], mb)
                wk2 = PFW.get(); proj(wk2[0], wk2[1], mb2)
                rotary(None, NM, T, 2, 3, blk)
                wv1 = PFW.get(); proj(wv1[0], wv1[1], mb)
                for (tok0, n, b) in mb:
                    evac_copy(b, fmb[4][:, tok0 - NM:tok0 - NM + n], banks[b][:, 0:n], [bres[b]], [fbr[4]])
                wv2 = PFW.get(); proj(wv2[0], wv2[1], mb2)
                for (tok0, n, b) in mb2:
                    evac_copy(b + 1, fmb[5][:, tok0 - NM:tok0 - NM + n], banks[b][:, 0:n], [bres[b]], [fbr[5]])
                wg1 = PFW.get(); proj(wg1[0], wg1[1], mb)
                for (tok0, n, b) in mb:
                    a = tok0 - NM
                    op("act", lambda e, a=a, n=n, b=b: e.activation(out=fmb[6][:, a:a + n], in_=banks[b][:, 0:n], func=AF.Silu), r=[bres[b]], w=[fbr[6]])
                wg2 = PFW.get(); proj(wg2[0], wg2[1], mb2)
                for (tok0, n, b) in mb2:
                    a = tok0 - NM
                    op("act", lambda e, a=a, n=n, b=b: e.activation(out=fmb[7][:, a:a + n], in_=banks[b][:, 0:n], func=AF.Silu), r=[bres[b]], w=[fbr[7]])
                to_tm([(fmb[4], fbr[4]), (fmb[5], fbr[5])], tchunks, tmv, r_tmv, 256)
                op("dve", lambda e: e.tensor_copy(out=S32[:, 0:512], in_=rstart[:, h * 512:(h + 1) * 512]), r=[R_start], w=[r_S])
                op("act", lambda e: e.copy(out=Sbf[:, 0:512], in_=rstart[:, h * 512:(h + 1) * 512]), r=[R_start], w=[r_Sbf])
                g128 = float(GAMMAS[h] ** 128)
                for ci, (a, n) in enumerate(tchunks):
                    q1, q2, k1, k2 = fmb[0], fmb[1], fmb[2], fmb[3]
                    op("pe", lambda e, a=a: e.matmul(banks[2][:, 0:128], lhsT=k1[:, a:a + 128], rhs=q1[:, a:a + 128], start=True, stop=False), r=[fbr[2], fbr[0]], w=[bres[2]])
                    op("pe", lambda e, a=a: e.matmul(banks[2][:, 0:128], lhsT=k2[:, a:a + 128], rhs=q2[:, a:a + 128], start=False, stop=True), r=[fbr[3], fbr[1]], w=[bres[2]])
                    op("dve", lambda e: e.tensor_tensor(out=scm[:, 0:128], in0=banks[2][:, 0:128], in1=rmask[:, h, :], op=ALU.mult), r=[bres[2], R_const], w=[r_scm])
                    op("pe", lambda e, ci=ci: e.matmul(banks[3][:, 0:256], lhsT=scm[:, 0:128], rhs=tmv[:, ci, 0:256], start=True, stop=False), r=[r_scm, r_tmv], w=[bres[3]])
                    op("pe", lambda e, a=a: e.matmul(banks[3][:, 0:256], lhsT=q1[:, a:a + 128], rhs=Sbf[:, 0:256], start=False, stop=False), r=[fbr[0], r_Sbf], w=[bres[3]])
                    op("pe", lambda e, a=a: e.matmul(banks[3][:, 0:256], lhsT=q2[:, a:a + 128], rhs=Sbf[:, 256:512], start=False, stop=True), r=[fbr[1], r_Sbf], w=[bres[3]])
                    if ci < NT - 1:
                        for j, kk_ in enumerate((k1, k2)):
                            op("pe", transpose_to(banks[7][:, j * 128:(j + 1) * 128], kk_[:, a:a + 128], 128), r=[fbr[2 + j], R_const], w=[bres[7]])
                        op("act", lambda e: e.activation(out=tmk[:, 0, 0:256], in_=banks[7][:, 0:256], func=AF.Copy, scale=kdec[:, h:h + 1]), r=[bres[7], R_const], w=[r_tmk])
                        for a_ in range(2):
                            op("pe", lambda e, a_=a_, ci=ci: e.matmul(banks[5][:, a_ * 256:(a_ + 1) * 256], lhsT=tmk[:, 0, a_ * 128:(a_ + 1) * 128], rhs=tmv[:, ci, 0:256], start=True, stop=True), r=[r_tmk, r_tmv], w=[bres[5]])
                        op("dve", lambda e: e.scalar_tensor_tensor(out=S32[:, 0:512], in0=S32[:, 0:512], scalar=g128, in1=banks[5][:, 0:512], op0=ALU.mult, op1=ALU.add), r=[bres[5], r_S], w=[r_S])
                        op("pool", lambda e: e.tensor_copy(out=Sbf[:, 0:512], in_=S32[:, 0:512]), r=[r_S], w=[r_Sbf])
                    op("dve", lambda e: e.bn_stats(out=cst[:, 0:6], in_=banks[3][:, 0:256]), r=[bres[3]], w=[r_c])
                    op("dve", lambda e: e.bn_aggr(out=cmv[:, 0:2], in_=cst[:, 0:6]), r=[r_c], w=[r_c])
                    op("act", lambda e: e.activation(out=crs[:, 1:2], in_=cmv[:, 1:2], func=AF.Ln, bias=epsr[:, h:h + 1]), r=[r_c, R_const], w=[r_c])
                    op("act", lambda e: e.activation(out=crs[:, 1:2], in_=crs[:, 1:2], func=AF.Exp, scale=-0.5), r=[r_c], w=[r_c])
                    op("dve", lambda e: e.tensor_scalar(out=xhc[:, 0:256], in0=banks[3][:, 0:256], scalar1=cmv[:, 0:1], scalar2=crs[:, 1:2], op0=ALU.subtract, op1=ALU.mult), r=[bres[3], r_c], w=[r_xhc])
                    for a_ in range(2):
                        op("pe", transpose_to(banks[4][:, a_ * 128:(a_ + 1) * 128], xhc[:, a_ * 128:(a_ + 1) * 128], 128), r=[r_xhc, R_const], w=[bres[4]])
                    for a_ in range(2):
                        idx = h * 2 + a_
                        op("dve", lambda e, a_=a_, idx=idx: e.tensor_scalar(out=tmp32[:, 0:128], in0=banks[4][:, a_ * 128:(a_ + 1) * 128], scalar1=rng_[:, idx:idx + 1], scalar2=rnb[:, idx:idx + 1], op0=ALU.mult, op1=ALU.add), r=[bres[4], R_const], w=[r_tmp])
                        op("pool", lambda e, a_=a_, idx=idx, a=a: e.tensor_tensor(out=mixT[:, 8 + idx, a:a + 128], in0=tmp32[:, 0:128], in1=fmb[6 + a_][:, a:a + 128], op=ALU.mult), r=[r_tmp, fbr[6 + a_]], w=[R_mix[8 + idx]])
            S.barrier()
            if DBG:
                dma("sp", "st", dbg_mix, mixT[:], w=[R_out])
                dma("sp", "st", dbg_start[:, 0:1024], sstart[:], w=[R_out])
                dma("sp", "st", dbg_start[:, 1024:3072], rstart[:], w=[R_out])
                dma("sp", "st", dbg_h0T, h0T[:], w=[R_out])
                S.barrier()
            stp2.close()
            stp1.close()
        stm.close()

        with contextlib.ExitStack() as st:
            y = sb(st, "y", [128, NT, D]); r_y = [Res() for _ in range(NT)]
            gB = sb(st, "gB", [128, D]); bB = sb(st, "bB", [128, D]); r_gb = Res()
            lnt = [(sb(st, f"l2st{i}", [128, 4, 6]), sb(st, f"l2mv{i}", [128, 2]), sb(st, f"l2rs{i}", [128, 2]), Res()) for i in range(2)]
            S.new_sem("gbld")
            with contextlib.ExitStack() as st4:
                xr = Ring(S, st4, nc, "x4t", 2, [128, D], F32, dma=True)
                WO = WLoader(S, st4, nc, "wo", KC, 512, 2, 2, kc_stage=4)
                dma("sp", "gbld", gB[:], lng_d.partition_broadcast(128), w=[r_gb])
                dma("sp", "gbld", bB[:], lnb_d.partition_broadcast(128), w=[r_gb])
                op("pool", lambda e: e.tensor_scalar(out=bB[:], in0=bB[:], scalar1=float(ALPHA), scalar2=None, op0=ALU.mult), r=[r_gb], w=[r_gb])
                for ti in range(NT):
                    xt, xres, xsem = xr.next()
                    dma("sp", xsem, xt[:], x_d[ti * 128:(ti + 1) * 128, :], w=[xres])
                    mv, rstd, rr = ln_stats(lnt[ti % 2], xt, 128, xres)
                    op("dve", lambda e: e.tensor_scalar(out=rstd[:, 0:1], in0=rstd[:, 0:1], scalar1=float(ALPHA), scalar2=None, op0=ALU.mult), r=[rr], w=[rr])
                    ln_apply(y[:, ti, :], xt[:], mv, rstd, rr, xres, r_y[ti], gB, bB, r_gb)
                wout_v = wout_d.rearrange("(kc p) n -> p kc n", p=128)
                PFO = Prefetch(WO, [wout_v[:, :, cb * 512:(cb + 1) * 512] for cb in range(4)], 1)
                for cb in range(4):
                    wt, wr = PFO.get()
                    for ti in range(NT):
                        b = (cb * NT + ti) % 4
                        for kc in range(KC):
                            op("pe", lambda e, kc=kc: e.matmul(banks[b][:, 0:512], lhsT=mixT[:, kc, ti * 128:(ti + 1) * 128], rhs=wt[:, kc, :], start=(kc == 0), stop=(kc == KC - 1)), r=[wr], w=[bres[b]])
                        op("dve", lambda e: e.tensor_tensor(out=y[:, ti, cb * 512:(cb + 1) * 512], in0=banks[b][:, 0:512], in1=y[:, ti, cb * 512:(cb + 1) * 512], op=ALU.add), r=[bres[b], r_y[ti]], w=[r_y[ti]])
                S.barrier()

            with contextlib.ExitStack() as st5:
                h1T = mixT; r_h1T = [Res() for _ in range(NT)]
                hbr = Ring(S, st5, nc, "hb", 2, [128, D], BF16)
                dma("sp", "gbld", gB[:], l1g_d.partition_broadcast(128), w=[r_gb])
                dma("sp", "gbld", bB[:], l1b_d.partition_broadcast(128), w=[r_gb])
                for ti in range(NT):
                    mv, rstd, rr = ln_stats(lnt[ti % 2], y[:, ti, :], 128, r_y[ti])
                    ln_apply(y[:, ti, :], y[:, ti, :], mv, rstd, rr, r_y[ti], r_y[ti], gB, bB, r_gb)
                    hb, hbres, _ = hbr.next()
                    op("act", lambda e: e.copy(out=hb[:], in_=y[:, ti, :]), r=[r_y[ti]], w=[hbres])
                    for g4 in range(4):
                        b = g4 % 2
                        for j in range(4):
                            dc = g4 * 4 + j
                            op("pe", transpose_to(banks[b][:, j * 128:(j + 1) * 128], hb[:, dc * 128:(dc + 1) * 128], 128), r=[hbres, R_const], w=[bres[b]])
                        dst = h1T[:, g4 * 4:(g4 + 1) * 4, ti * 128:(ti + 1) * 128]
                        srcp = banks[b][:, 0:512].rearrange("p (j t) -> p j t", t=128)
                        evac_copy(g4, dst, srcp, [bres[b]], [r_h1T[ti]])
                    op("act", lambda e: e.activation(out=y[:, ti, :], in_=y[:, ti, :], func=AF.Identity, scale=float(ALPHA)), r=[r_y[ti]], w=[r_y[ti]])
                WG = WLoader(S, st5, nc, "wgu", KC, 128, 2, 4)
                WD = WLoader(S, st5, nc, "wd", FG, 512, 2, 3)
                aT = [sb(st5, f"aT{i}", [128, FG, T], BF16) for i in range(2)]
                r_aT = [Res() for _ in range(2)]
                sg = [sb(st5, f"sg{i}", [128, HALF], BF16) for i in range(2)]
                r_sg = [Res() for _ in range(2)]
                wg_v = wg_d.rearrange("(kc p) n -> p kc n", p=128)
                wu_v = wu_d.rearrange("(kc p) n -> p kc n", p=128)
                wd_v = wd_d.rearrange("(fc p) n -> p fc n", p=128)
                k_sg = 0
                gspecs = []
                for fc in range(FC):
                    gspecs += [wg_v[:, :, fc * 128:(fc + 1) * 128], wu_v[:, :, fc * 128:(fc + 1) * 128]]
                PFG = Prefetch(WG, gspecs, 2)
                PFD = Prefetch(WD, [wd_v[:, g * FG:(g + 1) * FG, cb * 512:(cb + 1) * 512] for g in range(NG) for cb in range(4)], 1)
                for g in range(NG):
                    at, atr = aT[g % 2], r_aT[g % 2]
                    for f in range(FG):
                        fc = g * FG + f
                        wgt, wgr = PFG.get()
                        wut, wur = PFG.get()
                        for hf in range(NHALF):
                            bg_, bu_ = 0 + (hf % 2), 2 + (hf % 2)
                            tiles = list(range(hf * HALF // 128, (hf + 1) * HALF // 128))
                            for kc in range(KC):
                                op("pe", lambda e, kc=kc: e.matmul(banks[bg_][:, 0:HALF], lhsT=wgt[:, kc, :], rhs=h1T[:, kc, hf * HALF:(hf + 1) * HALF], start=(kc == 0), stop=(kc == KC - 1)), r=[wgr] + [r_h1T[t] for t in tiles], w=[bres[bg_]])
                            for kc in range(KC):
                                op("pe", lambda e, kc=kc: e.matmul(banks[bu_][:, 0:HALF], lhsT=wut[:, kc, :], rhs=h1T[:, kc, hf * HALF:(hf + 1) * HALF], start=(kc == 0), stop=(kc == KC - 1)), r=[wur] + [r_h1T[t] for t in tiles], w=[bres[bu_]])
                            s_, sr_ = sg[k_sg % 2], r_sg[k_sg % 2]
                            k_sg += 1
                            op("act", lambda e: e.activation(out=s_[:, 0:HALF], in_=banks[bg_][:, 0:HALF], func=AF.Silu), r=[bres[bg_]], w=[sr_])
                            op("dve", lambda e: e.tensor_tensor(out=at[:, f, hf * HALF:(hf + 1) * HALF], in0=banks[bu_][:, 0:HALF], in1=s_[:, 0:HALF], op=ALU.mult), r=[bres[bu_], sr_], w=[atr])
                    for cb in range(4):
                        wt, wr = PFD.get()
                        for ti in range(NT):
                            b = 4 + (cb * NT + ti) % 4
                            for f in range(FG):
                                op("pe", lambda e, f=f: e.matmul(banks[b][:, 0:512], lhsT=at[:, f, ti * 128:(ti + 1) * 128], rhs=wt[:, f, :], start=(f == 0), stop=(f == FG - 1)), r=[wr, atr], w=[bres[b]])
                            op("dve", lambda e: e.tensor_tensor(out=y[:, ti, cb * 512:(cb + 1) * 512], in0=banks[b][:, 0:512], in1=y[:, ti, cb * 512:(cb + 1) * 512], op=ALU.add), r=[bres[b], r_y[ti]], w=[r_y[ti]])
                dma("sp", "gbld", gB[:], l2g_d.partition_broadcast(128), r=[r_gb], w=[r_gb])
                dma("sp", "gbld", bB[:], l2b_d.partition_broadcast(128), r=[r_gb], w=[r_gb])
                for ti in range(NT):
                    mv, rstd, rr = ln_stats(lnt[ti % 2], y[:, ti, :], 128, r_y[ti])
                    ln_apply(y[:, ti, :], y[:, ti, :], mv, rstd, rr, r_y[ti], r_y[ti], gB, bB, r_gb)
                    dma("sp", "st", out_d[ti * 128:(ti + 1) * 128, :], y[:, ti, :], r=[r_y[ti]], w=[R_out])
                S.barrier()
        if ms_out is not None:
            ms_out.update(S.ms_rec)
    return nc


def build2(T, DFF, NCORES, DBG=False):
    ms = {}
    build(T, DFF, NCORES, DBG=DBG, ms_out=ms)
    return build(T, DFF, NCORES, DBG=DBG, milestones=ms)


def make_tables(c, T, NCORES):
    NT = T // 128
    TT = NM + T
    half = 128
    inv_freq = (np.float32(10000.0) ** (-np.arange(half, dtype=np.float32) / np.float32(half))).astype(np.float32)
    pos = np.concatenate([np.arange(NM), NM + c * T + np.arange(T)]).astype(np.float32)
    ang = (pos[None, :] * inv_freq[:, None]).astype(np.float32).astype(np.float64)
    cs = np.stack([np.cos(ang), np.sin(ang)], axis=1).astype(np.float32)
    g = np.array(GAMMAS, dtype=np.float64)
    s = np.arange(128, dtype=np.float64)
    rmask = np.zeros((128, RH, 128), np.float64)
    for h in range(RH):
        m = (s[None, :] >= s[:, None]).astype(np.float64) * (g[h] ** (-(s[:, None] + 1.0))) / 16.0
        rmask[:, h, :] = m
    kdec = (g[None, :] ** (127.0 - s[:, None])) / 16.0
    i_glob = (np.arange(NT)[None, :, None] * 128 + s[:, None, None])
    kdecf = (g[None, None, :] ** (T - 1.0 - i_glob)) / 16.0
    jm = np.arange(NM, dtype=np.float64)
    kdecm = (g[None, :] ** (NM - 1.0 - jm[:, None])) / 16.0
    epsr = HN_EPS * (g[None, :] ** (-2.0 * (s[:, None] + 1.0)))
    s64 = np.arange(64)
    hmask = (s64[None, :] >= s64[:, None]).astype(np.float32)
    rst = np.ones((128, TT), np.float32)
    rst[:, 0] = 0.0
    rst[:, NM::64] = 0.0
    sel = np.zeros((128, NCORES), np.float32)
    sel[:, c] = 1.0
    f32 = lambda a: np.ascontiguousarray(a, dtype=np.float32)
    return dict(cs=f32(cs), rmask=f32(rmask), kdec=f32(kdec), kdecf=f32(kdecf), kdecm=f32(kdecm),
                epsr=f32(epsr), hmask=f32(hmask), rst=f32(rst), sel=f32(sel),
                ident=np.eye(128, dtype=np.float32))


def make_in_maps(inp, T, NCORES):
    f = lambda a: np.ascontiguousarray(np.asarray(a, dtype=np.float32))
    x = f(inp["x"])[0]
    shared = dict(
        meta=f(inp["meta_tokens"]), ln_in_g=f(inp["ln_in_g"]), ln_in_b=f(inp["ln_in_b"]),
        lb2=f(inp["hg_lower_bounds"]), w_in=f(inp["w_in"])[0], hg_norm_g=f(inp["hg_norm_g"])[0],
        ret_norm_g=f(inp["ret_norm_g"])[0], ret_norm_b=f(inp["ret_norm_b"])[0], w_out=f(inp["w_out"])[0],
        ln1_g=f(inp["ln1_g"])[0], ln1_b=f(inp["ln1_b"])[0], w_gate=f(inp["w_gate"])[0],
        w_up=f(inp["w_up"])[0], w_down=f(inp["w_down"])[0], ln2_g=f(inp["ln2_g"])[0], ln2_b=f(inp["ln2_b"])[0])
    maps = []
    for c in range(NCORES):
        m = dict(shared)
        m["x"] = np.ascontiguousarray(x[c * T:(c + 1) * T])
        m.update(make_tables(c, T, NCORES))
        maps.append(m)
    return maps


def kernel(**inputs):
    NCORES = 8
    seq = inputs["x"].shape[1]
    T = seq // NCORES
    DFF = inputs["w_gate"].shape[2]
    nc = build2(T, DFF, NCORES)
    in_maps = make_in_maps(inputs, T, NCORES)
    res = run_bass_kernel_spmd(nc, in_maps, core_ids=list(range(NCORES)))
    out = np.concatenate([np.asarray(r["out"], dtype=np.float32) for r in res.results], axis=0)
    return out[None]
```
